# Optimizing a Trainium2 kernel written in Bass

```python
import jax, jax.numpy as jnp
from jax import lax
import numpy as np


D_MODEL = 1024
BATCH = 4
SEQ = 8192
DEPTH = 4

MLA_HEADS = 4
QK_NOPE_DIM = 64
QK_ROPE_DIM = 32
V_HEAD_DIM = 64
Q_RANK = D_MODEL // 4
KV_RANK = D_MODEL // 8
ROPE_BASE = 10000.0
Q_BLOCK = 128
SG_GROUPS = 4
SG_WIDTH = D_MODEL // 4
SG_CHUNK = 128
CONV_WIDTH = D_MODEL // 4
CONV_K = 3
POOL_WINDOWS = (2, 4, 8, 16)
POOL_WIDTH = D_MODEL // 4
POOL_GROUP = POOL_WIDTH // 4
N_BRANCH = 4
D_FF = 4 * D_MODEL
EPS = 1e-6
N_IN = Q_RANK + KV_RANK + QK_ROPE_DIM + 2 * SG_WIDTH + 3 * CONV_WIDTH + POOL_WIDTH + N_BRANCH * D_MODEL

kernel_name = 'hybrid_gated_mla_sgmlp_conv_pool_block'


def rmsnorm(x, g):
    xf = x.astype(jnp.float32)
    y = xf * lax.rsqrt(jnp.mean(xf * xf, axis=-1, keepdims=True) + EPS)
    return (y * g.astype(jnp.float32)).astype(x.dtype)


def layernorm(x, g, b):
    xf = x.astype(jnp.float32)
    mu = jnp.mean(xf, axis=-1, keepdims=True)
    xc = xf - mu
    y = xc * lax.rsqrt(jnp.mean(xc * xc, axis=-1, keepdims=True) + EPS)
    return (y * g.astype(jnp.float32) + b.astype(jnp.float32)).astype(x.dtype)


def split_cols(proj):
    sizes = (Q_RANK, KV_RANK, QK_ROPE_DIM, SG_WIDTH, SG_WIDTH, CONV_WIDTH, CONV_WIDTH, CONV_WIDTH,
             POOL_WIDTH, N_BRANCH * D_MODEL)
    offs = []
    acc = 0
    for s in sizes[:-1]:
        acc += s
        offs.append(acc)
    return jnp.split(proj, offs, axis=-1)


def rope(x, cos, sin):
    x1, x2 = jnp.split(x, 2, axis=-1)
    return jnp.concatenate([x1 * cos - x2 * sin, x2 * cos + x1 * sin], axis=-1)


def mla(c_q, c_kv, k_rope, positions, q_norm, w_uq, kv_norm, w_ukv):
    B_, S_, _ = c_q.shape
    q = (rmsnorm(c_q, q_norm) @ w_uq).reshape(B_, S_, MLA_HEADS, QK_NOPE_DIM + QK_ROPE_DIM)
    q_nope, q_rope = q[..., :QK_NOPE_DIM], q[..., QK_NOPE_DIM:]
    kv = (rmsnorm(c_kv, kv_norm) @ w_ukv).reshape(B_, S_, MLA_HEADS, QK_NOPE_DIM + V_HEAD_DIM)
    k_nope, v = kv[..., :QK_NOPE_DIM], kv[..., QK_NOPE_DIM:]
    inv_freq = ROPE_BASE ** (-jnp.arange(0, QK_ROPE_DIM, 2, dtype=jnp.float32) / QK_ROPE_DIM)
    ang = positions.astype(jnp.float32)[..., None] * inv_freq
    cos = jnp.cos(ang).astype(q.dtype)
    sin = jnp.sin(ang).astype(q.dtype)
    q_rope = rope(q_rope, cos[:, :, None, :], sin[:, :, None, :])
    k_rope = rope(k_rope, cos, sin)
    scale = (QK_NOPE_DIM + QK_ROPE_DIM) ** -0.5
    nb = S_ // Q_BLOCK

    def to_blocks(a):
        return jnp.moveaxis(a.reshape((B_, nb, Q_BLOCK) + a.shape[2:]), 1, 0)

    k_idx = jnp.arange(S_)

    def block(args):
        qn, qr, start = args
        s = jnp.einsum('bqhd,bkhd->bhqk', qn, k_nope) + jnp.einsum('bqhr,bkr->bhqk', qr, k_rope)
        s = s.astype(jnp.float32) * scale
        q_idx = start + jnp.arange(Q_BLOCK)
        s = jnp.where(k_idx[None, :] <= q_idx[:, None], s, -jnp.inf)
        p = jax.nn.softmax(s, axis=-1).astype(v.dtype)
        return jnp.einsum('bhqk,bkhd->bqhd', p, v)

    starts = jnp.arange(nb, dtype=jnp.int32) * Q_BLOCK
    o = lax.map(block, (to_blocks(q_nope), to_blocks(q_rope), starts))
    return jnp.moveaxis(o, 0, 1).reshape(B_, S_, MLA_HEADS * V_HEAD_DIM)


def spatial_gating(u, v, ln_g, ln_b, w_s, b_s):
    B_, S_, _ = u.shape
    u = jax.nn.gelu(u)
    v = layernorm(jax.nn.gelu(v), ln_g, ln_b)
    n = S_ // SG_CHUNK
    vc = v.reshape(B_, n, SG_CHUNK, SG_GROUPS, SG_WIDTH // SG_GROUPS)
    mask = jnp.tril(jnp.ones((SG_CHUNK, SG_CHUNK), dtype=bool))
    w = jnp.where(mask, w_s, 0)
    s = jnp.einsum('gts,bnsgc->bntgc', w, vc) + b_s.T[:, :, None]
    return u * s.reshape(B_, S_, SG_WIDTH)


def short_conv(xin, bg, cg, conv_w):
    z = cg * xin
    S_ = z.shape[1]
    zp = jnp.pad(z, ((0, 0), (CONV_K - 1, 0), (0, 0)))
    y = sum(conv_w[k] * zp[:, k:k + S_] for k in range(CONV_K))
    return bg * y


def multiscale_pool(p, w_pool, scale):
    B_, S_, _ = p.shape
    pg = p.reshape(B_, S_, len(POOL_WINDOWS), POOL_GROUP).astype(jnp.float32)
    cs = jnp.cumsum(pg, axis=1)
    t = jnp.arange(S_)
    outs = []
    for g, win in enumerate(POOL_WINDOWS):
        c = cs[:, :, g]
        lagged = jnp.pad(c, ((0, 0), (win, 0), (0, 0)))[:, :S_]
        cnt = jnp.minimum(t + 1, win).astype(jnp.float32)[None, :, None]
        outs.append((c - lagged) / cnt - pg[:, :, g])
    pooled = jnp.stack(outs, axis=2).astype(p.dtype)
    mixed = jnp.einsum('bsgc,gcd->bsgd', pooled, w_pool)
    return mixed.reshape(B_, S_, POOL_WIDTH) * scale


def setup_inputs(seed: int = 0) -> dict:
    key = jax.random.key(seed)
    ks = jax.random.split(key, 26)
    L = DEPTH

    def nrm(k, shape, fan_in):
        return jax.random.normal(k, shape, jnp.float32) * fan_in ** -0.5

    def gain(k, shape):
        return 1.0 + 0.05 * jax.random.normal(k, shape, jnp.float32)

    def small(k, shape, s):
        return s * jax.random.normal(k, shape, jnp.float32)

    x = jax.random.normal(ks[0], (BATCH, SEQ, D_MODEL), jnp.float32)
    positions = (jax.random.randint(ks[1], (BATCH, 1), 0, 4096, dtype=jnp.int32)
                 + jnp.arange(SEQ, dtype=jnp.int32)[None, :])
    return {
        'x': x,
        'positions': positions,
        'norm_mix_pre': gain(ks[2], (L, D_MODEL)),
        'w_in': nrm(ks[3], (L, D_MODEL, N_IN), D_MODEL),
        'gate_b': small(ks[4], (L, N_BRANCH * D_MODEL), 0.01),
        'q_norm': gain(ks[5], (L, Q_RANK)),
        'w_uq': nrm(ks[6], (L, Q_RANK, MLA_HEADS * (QK_NOPE_DIM + QK_ROPE_DIM)), Q_RANK),
        'kv_norm': gain(ks[7], (L, KV_RANK)),
        'w_ukv': nrm(ks[8], (L, KV_RANK, MLA_HEADS * (QK_NOPE_DIM + V_HEAD_DIM)), KV_RANK),
        'w_br_mla': nrm(ks[9], (L, MLA_HEADS * V_HEAD_DIM, D_MODEL), MLA_HEADS * V_HEAD_DIM),
        'sg_ln_g': gain(ks[10], (L, SG_WIDTH)),
        'sg_ln_b': small(ks[11], (L, SG_WIDTH), 0.02),
        'sg_w': nrm(ks[12], (L, SG_GROUPS, SG_CHUNK, SG_CHUNK), SG_CHUNK),
        'sg_b': gain(ks[13], (L, SG_GROUPS, SG_CHUNK)),
        'w_br_sg': nrm(ks[14], (L, SG_WIDTH, D_MODEL), SG_WIDTH),
        'conv_w': nrm(ks[15], (L, CONV_K, CONV_WIDTH), CONV_K),
        'w_br_conv': nrm(ks[16], (L, CONV_WIDTH, D_MODEL), CONV_WIDTH),
        'pool_w': nrm(ks[17], (L, len(POOL_WINDOWS), POOL_GROUP, POOL_GROUP), POOL_GROUP),
        'pool_scale': gain(ks[18], (L, POOL_WIDTH)),
        'w_br_pool': nrm(ks[19], (L, POOL_WIDTH, D_MODEL), POOL_WIDTH),
        'w_out': nrm(ks[20], (L, D_MODEL, D_MODEL), D_MODEL),
        'norm_mix_post': gain(ks[21], (L, D_MODEL)),
        'norm_ffn_pre': gain(ks[22], (L, D_MODEL)),
        'w_ff1': nrm(ks[23], (L, D_MODEL, D_FF), D_MODEL),
        'w_ff2': nrm(ks[24], (L, D_FF, D_MODEL), D_FF),
        'norm_ffn_post': gain(ks[25], (L, D_MODEL)),
    }


def reference(x, positions, norm_mix_pre, w_in, gate_b, q_norm, w_uq, kv_norm, w_ukv, w_br_mla,
              sg_ln_g, sg_ln_b, sg_w, sg_b, w_br_sg, conv_w, w_br_conv, pool_w, pool_scale, w_br_pool,
              w_out, norm_mix_post, norm_ffn_pre, w_ff1, w_ff2, norm_ffn_post):
    B_, S_, D_ = x.shape
    for l in range(DEPTH):
        h = rmsnorm(x, norm_mix_pre[l])
        (c_q, c_kv, k_r, sg_u, sg_v, cv_x, cv_b, cv_c, pool_in, gate_pre) = split_cols(h @ w_in[l])
        gates = jax.nn.sigmoid(gate_pre + gate_b[l]).reshape(B_, S_, N_BRANCH, D_)
        y_a = mla(c_q, c_kv, k_r, positions, q_norm[l], w_uq[l], kv_norm[l], w_ukv[l]) @ w_br_mla[l]
        y_b = spatial_gating(sg_u, sg_v, sg_ln_g[l], sg_ln_b[l], sg_w[l], sg_b[l]) @ w_br_sg[l]
        y_c = short_conv(cv_x, cv_b, cv_c, conv_w[l]) @ w_br_conv[l]
        y_d = multiscale_pool(pool_in, pool_w[l], pool_scale[l]) @ w_br_pool[l]
        merged = (gates[:, :, 0] * y_a + gates[:, :, 1] * y_b
                  + gates[:, :, 2] * y_c + gates[:, :, 3] * y_d)
        x = x + rmsnorm(merged @ w_out[l], norm_mix_post[l])
        h = rmsnorm(x, norm_ffn_pre[l])
        f = jnp.square(jax.nn.relu(h @ w_ff1[l])) @ w_ff2[l]
        x = x + rmsnorm(f, norm_ffn_post[l])
    return x
```

```python
import contextlib
import math
import numpy as np
import concourse.bass as bass
import concourse.mybir as mybir
from concourse.bass_utils import run_bass_kernel_spmd

F32 = mybir.dt.float32
BF16 = mybir.dt.bfloat16
I32 = mybir.dt.int32
AF = mybir.ActivationFunctionType
ALU = mybir.AluOpType
AX = mybir.AxisListType

D_MODEL = 1024
DEPTH = 4
SEQ = 8192
BATCH = 4
NB = 512
EPS = 1e-6
NSLOT = 34
SLOTW = 4096
NPAR = 80
NMISC = 2304
ATT_SCALE = 96.0 ** -0.5
GC1 = 0.7978845608028654
GC2 = 0.044715
TWO_PI = 2.0 * math.pi
CW1 = 6.28125
CW2 = TWO_PI - CW1
PI_SAFE = 3.1415925

ENGS = ["tensor", "vector", "scalar", "gpsimd", "sync"]
PAGE = 512


class _Op:
    __slots__ = ("eng", "fn", "deps", "is_dma", "slot", "seq", "sig", "users")

    def __init__(self, eng, fn, slot, seq):
        self.eng = eng
        self.fn = fn
        self.deps = []
        self.is_dma = slot is not None
        self.slot = slot
        self.seq = seq
        self.sig = None
        self.users = 0


def _keys_of(x):
    if not isinstance(x, bass.AP):
        return [x]
    ap = x.ap
    esz = 2 if x.dtype == BF16 else 4
    pstride = ap[0][0]
    off = x.offset
    fo = off % pstride if pstride > 0 else off
    ext = 0
    for st, cnt in ap[1:]:
        ext += abs(st) * (cnt - 1)
    b0 = fo * esz
    b1 = (fo + ext + 1) * esz - 1
    name = x.tensor.name
    return [(name, p) for p in range(b0 // PAGE, b1 // PAGE + 1)]


class Sched:
    def __init__(self, nc):
        self.nc = nc
        self.streams = {e: [] for e in ENGS}
        self.last_w = {}
        self.readers = {}
        self.last_dma = {}
        self.seq = 0

    def op(self, eng, fn, reads=(), writes=(), dma=None):
        self.seq += 1
        o = _Op(eng, fn, dma, self.seq)
        cand = {}

        def add(d):
            if d is None:
                return
            k = ("s", d.slot) if d.is_dma else ("e", d.eng)
            c = cand.get(k)
            if c is None or c.seq < d.seq:
                cand[k] = d

        rk = []
        for r in reads:
            rk.extend(_keys_of(r))
        wk = []
        for w in writes:
            wk.extend(_keys_of(w))
        for k in rk:
            add(self.last_w.get(k))
        for k in wk:
            add(self.last_w.get(k))
            rd = self.readers.get(k)
            if rd:
                for r in rd.values():
                    add(r)
        if dma is not None:
            add(self.last_dma.get(dma))
            self.last_dma[dma] = o
        for d in cand.values():
            if eng == "tensor" and (not d.is_dma) and d.eng == "tensor":
                continue
            o.deps.append(d)
            d.users += 1
        me = ("s", dma) if dma is not None else ("e", eng)
        for k in rk:
            rd = self.readers.get(k)
            if rd is None:
                rd = {}
                self.readers[k] = rd
            rd[me] = o
        for k in wk:
            self.last_w[k] = o
            self.readers[k] = {}
        self.streams[eng].append(o)
        return o

    def emit(self, final_wait_slots=()):
        nc = self.nc
        eng_cnt = {e: 0 for e in ENGS}
        slot_cnt = {}
        allops = []
        for e in ENGS:
            allops.extend(self.streams[e])
        allops.sort(key=lambda o: o.seq)
        for o in allops:
            if o.is_dma:
                slot_cnt[o.slot] = slot_cnt.get(o.slot, 0) + 16
                o.sig = slot_cnt[o.slot]
            elif o.users > 0:
                eng_cnt[o.eng] += 1
                o.sig = eng_cnt[o.eng]
        with contextlib.ExitStack() as st:
            esem = {e: st.enter_context(nc.semaphore("c_" + e)) for e in ENGS}
            ssem = {s: st.enter_context(nc.semaphore("d_" + str(s))) for s in slot_cnt}
            block = st.enter_context(nc.Block())

            def run_stream(e, engobj):
                known = {}
                for o in self.streams[e]:
                    need = {}
                    for d in o.deps:
                        key = ("s", d.slot) if d.is_dma else ("e", d.eng)
                        if need.get(key, 0) < d.sig:
                            need[key] = d.sig
                    for key, v in need.items():
                        if known.get(key, 0) >= v:
                            continue
                        known[key] = v
                        sem = ssem[key[1]] if key[0] == "s" else esem[key[1]]
                        engobj.wait_ge(sem, v)
                    ins = o.fn(engobj)
                    if o.is_dma:
                        ins.then_inc(ssem[o.slot], 16)
                    elif o.sig is not None:
                        ins.then_inc(esem[e], 1)
                if e == "sync":
                    for s in final_wait_slots:
                        if s in slot_cnt:
                            engobj.wait_ge(ssem[s], slot_cnt[s])

            @block.tensor
            def _(eng):
                run_stream("tensor", eng)

            @block.vector
            def _(eng):
                run_stream("vector", eng)

            @block.scalar
            def _(eng):
                run_stream("scalar", eng)

            @block.gpsimd
            def _(eng):
                run_stream("gpsimd", eng)

            @block.sync
            def _(eng):
                run_stream("sync", eng)


def build_program(T, L, dbg=None):
    NBLK = T // NB
    nc = bass.Bass("TRN2", target_bir_lowering=False)
    dt_in = lambda n, s, d: nc.dram_tensor(n, s, d, kind="ExternalInput").ap()
    xT = dt_in("xT", [D_MODEL, T], F32)
    pos = dt_in("pos", [1, T], I32)
    wsl = dt_in("wsl", [L * NSLOT * 128, SLOTW], F32)
    par_d = dt_in("par_in", [L * 128, NPAR], F32)
    misc_d = dt_in("misc_in", [L * 128, NMISC], F32)
    lnp_d = dt_in("lnp_in", [L * 128, 768], F32)
    cst_d = dt_in("cst_in", [128, 8], F32)
    outT = nc.dram_tensor("outT", [D_MODEL, T], F32, kind="ExternalOutput").ap()
    wscr = nc.dram_tensor("wscr", [L * NSLOT * 128, SLOTW], BF16, kind="Internal").ap()
    kscr = nc.dram_tensor("kscr", [L * 4 * 96, T], BF16, kind="Internal").ap()
    vscr = nc.dram_tensor("vscr", [L * NBLK * 128, 1024], BF16, kind="Internal").ap()
    csscr = nc.dram_tensor("csscr", [128, 2 * T], F32, kind="Internal").ap()
    mscr = nc.dram_tensor("mscr", [L * 128, NMISC], BF16, kind="Internal").ap()

    S = Sched(nc)
    st = contextlib.ExitStack()
    with st:
        def sb(name, shape, dt):
            return st.enter_context(nc.sbuf_tensor(name, shape, dt))

        ps = st.enter_context(nc.psum_tensor("ps", [128, 8, 512], F32))

        xres = sb("xres", [128, 8, 512], F32)
        xn = sb("xn", [128, 8, 512], BF16)
        sq = sb("sq", [128, 8, 512], BF16)
        NRING = 4
        wring = sb("wring", [128, NRING, SLOTW], BF16)
        cs = sb("cs", [128, 2, 512], F32)
        par = sb("par", [128, L, NPAR], F32)
        hgb = sb("hgb", [128, L, 32], F32)
        mwt = sb("mwt", [128, NMISC], BF16)
        lnp = sb("lnp", [128, 768], F32)
        cst = sb("cst", [128, 8], F32)
        ones = sb("ones", [128, 128], BF16)
        tri = sb("tri", [128, 128], BF16)
        epsA = sb("epsA", [128, 1], F32)
        eps4 = sb("eps4", [128, 1], F32)
        zh = sb("zh", [128, L, 2, 2], F32)
        ph = sb("ph", [128, L, 2, 15], F32)
        icn = sb("icn", [128, 2, 16], F32)
        Rt = sb("Rt", [128, 4, 512], F32)
        FA = sb("FA", [128, 17 * 1024], F32)
        BA = sb("BA", [128, 17 * 1024], BF16)

        def fa(off_k, shape):
            n = int(np.prod(shape))
            v = FA[:, off_k:off_k + n]
            if len(shape) == 1:
                return v
            if len(shape) == 2:
                return v.rearrange("p (a b) -> p a b", b=shape[1])
            return v.rearrange("p (a b c) -> p a b c", b=shape[1], c=shape[2])

        def ba(off_k, shape):
            n = int(np.prod(shape))
            v = BA[:, off_k:off_k + n]
            if len(shape) == 1:
                return v
            if len(shape) == 2:
                return v.rearrange("p (a b) -> p a b", b=shape[1])
            return v.rearrange("p (a b c) -> p a b c", b=shape[1], c=shape[2])

        K = 1024
        macc = fa(0, [8, 512])
        zf = macc
        tht = fa(4 * K, [2, 512])
        gtmp = fa(5 * K, [2, 512])
        rl = fa(6 * K, [2, 512])
        cqf = fa(0, [2, 512])
        ckvf = fa(1 * K, [512])
        krt1 = fa(1 * K + 512, [512])
        krA = fa(2 * K, [512])
        qr1 = fa(2 * K + 512, [512])
        qr2 = fa(3 * K, [512])
        qrb = fa(3 * K + 512, [512])
        xh = fa(4 * K, [2, 512])
        gu = fa(5 * K, [2, 512])
        gt = fa(6 * K, [2, 512])
        V1 = fa(7 * K, [4, 256])
        V2 = fa(8 * K, [4, 256])
        V3 = fa(9 * K, [4, 256])
        cvc = fa(10 * K, [2, 512])
        zt = fa(11 * K, [2, 514])
        p0 = fa(12 * K + 8, [2, 527])
        pA = fa(13 * K + 64, [2, 527])
        pB = fa(14 * K + 128, [2, 527])
        sgt = fa(15 * K + 256, [512])
        rden = fa(16 * K, [512])
        vst = fa(16 * K + 512, [4, 8])
        hid = ba(0, [32, 512])
        mbf = ba(11 * K, [8, 512])
        NKV = 3
        kbuf = ba(0, [NKV, 512])
        NPT = 3
        ptile = ba(4 * K, [NPT, 512])
        qT = ba(6 * K, [4, 512])
        qn = ba(10 * K, [2, 512])
        ckvn = ba(11 * K, [512])
        sqk = ba(11 * K + 512, [512])
        sq2 = ba(12 * K, [2, 512])
        krb = ba(13 * K, [512])
        knb = ba(14 * K, [2, 512])
        vsb = ba(8 * K, [1024]).rearrange("p (h c d) -> p h c d", c=4, d=64)
        oT = ba(15 * K, [4, 512])
        vbuf = sb("vbuf", [128, NKV, 4, 128], BF16)
        vnA = sb("vnA", [128, 4, 4, 128], BF16)
        brin = sb("brin", [128, 8, 512], BF16)
        ysg = brin[:, 0:2, :]
        ycv = brin[:, 2:4, :]
        ypl = brin[:, 4:6, :]
        pooled = brin[:, 6:8, :]

        gp_banks = [0, 1, 2]
        s_banks = [3, 4, 5]
        o_banks = [6, 7]
        rot = {"gp": 0, "s": 0, "pt": 0, "kv": 0, "th": 0, "gt": 0, "rl": 0, "st": 0}

        def nxt(name, n):
            v = rot[name]
            rot[name] = (v + 1) % n
            return v

        def gbank():
            return ps[:, gp_banks[nxt("gp", len(gp_banks))], :]

        def MM(out, lhsT, rhs, start, stop):
            rd = [lhsT, rhs] if start else [lhsT, rhs, out]
            S.op("tensor", lambda e: e.matmul(out, lhsT, rhs, start=start, stop=stop),
                 reads=rd, writes=[out])

        def ACT(out, in_, func, scale=1.0, bias=None, extra_reads=()):
            rd = [in_] + list(extra_reads)
            if isinstance(scale, bass.AP):
                rd.append(scale)
            if bias is None:
                S.op("scalar", lambda e: e.activation(out=out, in_=in_, func=func, scale=scale),
                     reads=rd, writes=[out])
            else:
                rd.append(bias)
                S.op("scalar", lambda e: e.activation(out=out, in_=in_, func=func, scale=scale, bias=bias),
                     reads=rd, writes=[out])

        def TT(eng, out, in0, in1, op):
            S.op(eng, lambda e: e.tensor_tensor(out=out, in0=in0, in1=in1, op=op),
                 reads=[in0, in1], writes=[out])

        def TS(eng, out, in0, s1, op0, s2=None, op1=None):
            rd = [in0]
            if isinstance(s1, bass.AP):
                rd.append(s1)
            if isinstance(s2, bass.AP):
                rd.append(s2)
            if op1 is None:
                S.op(eng, lambda e: e.tensor_scalar(out=out, in0=in0, scalar1=s1, scalar2=None, op0=op0),
                     reads=rd, writes=[out])
            else:
                S.op(eng, lambda e: e.tensor_scalar(out=out, in0=in0, scalar1=s1, scalar2=s2, op0=op0, op1=op1),
                     reads=rd, writes=[out])

        def STT(eng, out, in0, scalar, in1, op0, op1):
            eng = "vector"
            rd = [in0, in1]
            if isinstance(scalar, bass.AP):
                rd.append(scalar)
            S.op(eng, lambda e: e.scalar_tensor_tensor(out=out, in0=in0, scalar=scalar, in1=in1, op0=op0, op1=op1),
                 reads=rd, writes=[out])

        def CP(eng, out, in_):
            if eng == "scalar":
                ACT(out, in_, AF.Copy)
            else:
                S.op(eng, lambda e: e.tensor_copy(out=out, in_=in_), reads=[in_], writes=[out])

        def MEMSET(eng, out, val):
            S.op(eng, lambda e: e.memset(out, val), writes=[out])

        def RECIP(out, in_):
            S.op("vector", lambda e: e.reciprocal(out=out, in_=in_), reads=[in_], writes=[out])

        def DMA(eng, out, in_, slot, reads=(), writes=()):
            S.op(eng, lambda e: e.dma_start(out=out, in_=in_), reads=list(reads), writes=list(writes), dma=slot)

        def LOAD(eng, out, in_, slot, dkey=None):
            DMA(eng, out, in_, slot, reads=([dkey] if dkey is not None else []), writes=[out])

        def STORE(eng, out, in_, slot, dkey=None):
            DMA(eng, out, in_, slot, reads=[in_], writes=([dkey] if dkey is not None else []))

        dbg_n = [0]

        def DBG(name, ap, i, l):
            if dbg is None or (i, l) != dbg:
                return
            shp = list(ap.shape)
            d = nc.dram_tensor("dbg_" + name, shp, ap.dtype, kind="ExternalOutput").ap()
            dbg_n[0] += 1
            STORE("sync", d, ap, "dbg%d" % (dbg_n[0] % 2))

        NSTS = 4

        def st_slot():
            return "st%d" % nxt("st", NSTS)

        LOAD("sync", cst[:], cst_d, "su0")
        for l in range(L):
            LOAD("sync", par[:, l, :], par_d[l * 128:(l + 1) * 128, :], "su1")
        MEMSET("vector", ones[:], 1.0)
        MEMSET("vector", epsA[:], EPS)
        MEMSET("vector", eps4[:], 4.0 * EPS)
        MEMSET("vector", zh[:], 0.0)
        MEMSET("vector", ph[:], 0.0)
        MEMSET("gpsimd", BA[:], 0.0)
        MEMSET("gpsimd", FA[:], 0.0)
        trf = Rt[:, 0, 0:128]
        MEMSET("gpsimd", trf, 1.0)
        S.op("gpsimd", lambda e: e.affine_select(out=trf, in_=trf, pattern=[[1, 128]], compare_op=ALU.is_ge,
                                                 fill=0.0, base=0, channel_multiplier=-1),
             reads=[trf], writes=[trf])
        CP("vector", tri[:], trf)
        MEMSET("gpsimd", vnA[:], 0.0)
        MEMSET("gpsimd", vbuf[:], 0.0)
        for s_ in range(NKV):
            MEMSET("gpsimd", vbuf[:, s_, :, 64:128], 1.0)
        for l in range(L):
            TS("vector", hgb[:, l, :], par[:, l, 8:40], 0.5, ALU.mult)
            LOAD("gpsimd", mwt[:], misc_d[l * 128:(l + 1) * 128, :], "su2")
            for g in range(4):
                w_ = mwt[:, 1536 + g * 128:1536 + (g + 1) * 128]
                TT("gpsimd", w_, w_, tri[:], ALU.mult)
            STORE("sync", mscr[l * 128:(l + 1) * 128, :], mwt[:], st_slot(), dkey=("m", l))
        ici = Rt[:, 1, 0:16].bitcast(I32)
        S.op("gpsimd", lambda e: e.iota(ici, pattern=[[1, 16]], base=1, channel_multiplier=0),
             writes=[ici])
        icf = Rt[:, 2, 0:16]
        CP("vector", icf, ici)
        for a in range(2):
            TS("vector", icn[:, a, :], icf, cst[:, 2 + a:3 + a], ALU.min)
            RECIP(icn[:, a, :], icn[:, a, :])
        posi = Rt[:, 0, :].bitcast(I32)
        ang = Rt[:, 1, :]
        kk = Rt[:, 2, :]
        mm_ = Rt[:, 3, :]
        cstile = cs
        for c in range(NBLK):
            sl = slice(c * NB, (c + 1) * NB)
            LOAD("sync", posi, pos[0:1, sl].broadcast_to([128, NB]), "su2")
            CP("vector", ang, posi)
            TS("vector", ang, ang, cst[:, 0:1], ALU.mult)
            TS("vector", kk, ang, 1.0 / TWO_PI, ALU.mult, 12582912.0, ALU.add)
            TS("vector", kk, kk, 12582912.0, ALU.subtract)
            STT("vector", ang, kk, -CW1, ang, ALU.mult, ALU.add)
            STT("vector", ang, kk, -CW2, ang, ALU.mult, ALU.add)
            for sgn in (1.0, -1.0):
                if sgn > 0:
                    TS("vector", mm_, ang, math.pi, ALU.is_gt, -TWO_PI, ALU.mult)
                else:
                    TS("vector", mm_, ang, -math.pi, ALU.is_lt, TWO_PI, ALU.mult)
                TT("vector", ang, ang, mm_, ALU.add)
            TS("vector", kk, ang, PI_SAFE, ALU.min, -PI_SAFE, ALU.max)
            ACT(cstile[:, 1, :], kk, AF.Sin)
            TS("vector", cstile[:, 1, :], cstile[:, 1, :], cst[:, 1:2], ALU.mult)
            TS("vector", kk, ang, math.pi / 2.0, ALU.add)
            TS("vector", mm_, kk, math.pi, ALU.is_gt, -TWO_PI, ALU.mult)
            TT("vector", kk, kk, mm_, ALU.add)
            TS("vector", kk, kk, PI_SAFE, ALU.min, -PI_SAFE, ALU.max)
            ACT(cstile[:, 0, :], kk, AF.Sin)
            STORE("sync", csscr[:, c * NB:(c + 1) * NB], cstile[:, 0, :], st_slot(), dkey=("cs", c))
            STORE("sync", csscr[:, T + c * NB:T + (c + 1) * NB], cstile[:, 1, :], st_slot(), dkey=("cs", c))

        wst = {"next": 0}

        def slot_region(s):
            if s == 4:
                return 64, SLOTW
            if s in (7, 10, 13):
                return 128, 2048
            return 128, SLOTW

        def issue_loads_upto(n):
            while wst["next"] <= n:
                m = wst["next"]
                bi_, rem = divmod(m, L * NSLOT)
                l_, s_ = divmod(rem, NSLOT)
                buf = wring[:, m % NRING, :]
                row0 = (l_ * NSLOT + s_) * 128
                if bi_ == 0:
                    LOAD("gpsimd", buf, wsl[row0:row0 + 128, :], "wl%d" % (m % NRING))
                    STORE("sync", wscr[row0:row0 + 128, :], buf, "ws%d" % (m % 2), dkey=("w", l_, s_))
                else:
                    pp, cc = slot_region(s_)
                    LOAD("sync", wring[0:pp, m % NRING, 0:cc], wscr[row0:row0 + pp, 0:cc],
                         "wl%d" % (m % NRING), dkey=("w", l_, s_))
                wst["next"] += 1

        def W(i, l, s, keep=0):
            n = (i * L + l) * NSLOT + s
            issue_loads_upto(min(n + NRING - 1 - keep, NBLK * L * NSLOT - 1))
            return wring[:, n % NRING, :]

        def rms_to_bf16(src, gcol0, l, dst, eps_t):
            for c in range(8):
                if c % 2 == 0:
                    ACT(sq[:, c, :], src[:, c, :], AF.Square)
                else:
                    TT("gpsimd", sq[:, c, :], src[:, c, :], src[:, c, :], ALU.mult)
            bk = gbank()
            for c in range(8):
                MM(bk, ones[:], sq[:, c, :], c == 0, c == 7)
            ACT(Rt[:, 0, :], bk, AF.Sqrt, scale=1.0 / D_MODEL, bias=eps_t[:])
            RECIP(Rt[:, 1, :], Rt[:, 0, :])
            for c in range(8):
                STT("vector" if c % 2 == 0 else "gpsimd", dst[:, c, :], src[:, c, :],
                    par[:, l, gcol0 + c:gcol0 + c + 1], Rt[:, 1, :], ALU.mult, ALU.mult)

        def post_norm_residual(l, gcol0, eps_t):
            bk = gbank()
            for c in range(8):
                MM(bk, ones[:], sq[:, c, :], c == 0, c == 7)
            ACT(Rt[:, 0, :], bk, AF.Sqrt, scale=1.0 / D_MODEL, bias=eps_t[:])
            RECIP(Rt[:, 1, :], Rt[:, 0, :])
            for c in range(8):
                tmp = gtmp[:, c % 2, :]
                STT("vector", tmp, zf[:, c, :], par[:, l, gcol0 + c:gcol0 + c + 1], Rt[:, 1, :],
                    ALU.mult, ALU.mult)
                TT("gpsimd", xres[:, c, :], xres[:, c, :], tmp, ALU.add)

        def gelu_half(xh_ap, tmp_ap, out_ap):
            ACT(tmp_ap, xh_ap, AF.Square)
            TS("vector", tmp_ap, tmp_ap, 4.0 * GC2, ALU.mult, 1.0, ALU.add)
            TT("gpsimd", tmp_ap, tmp_ap, xh_ap, ALU.mult)
            ACT(tmp_ap, tmp_ap, AF.Tanh, scale=2.0 * GC1)
            STT("vector", out_ap, tmp_ap, 1.0, xh_ap, ALU.add, ALU.mult)

        for i in range(NBLK):
            blk = slice(i * NB, (i + 1) * NB)
            LOAD("sync", xres[:], xT.rearrange("(c p) t -> p c t", p=128)[:, :, blk], "xl")
            LOAD("sync", cs[:, 0, :], csscr[:, i * NB:(i + 1) * NB], "cl0", dkey=("cs", i))
            LOAD("sync", cs[:, 1, :], csscr[:, T + i * NB:T + (i + 1) * NB], "cl1", dkey=("cs", i))
            for l in range(L):
                P = lambda c0: par[:, l, c0:c0 + 1]
                LOAD("sync", mwt[:], mscr[l * 128:(l + 1) * 128, :], "ml", dkey=("m", l))
                LOAD("sync", lnp[:], lnp_d[l * 128:(l + 1) * 128, :], "pl")
                rms_to_bf16(xres, 0, l, xn, epsA)
                CP("gpsimd", zt[:, :, 0:2], zh[:, l, :, :])
                CP("gpsimd", p0[:, :, 0:15], ph[:, l, :, :])
                def wchunk(q):
                    wt = W(i, l, q // 4)
                    return wt[:, (q % 4) * 1024:(q % 4 + 1) * 1024].rearrange("p (k o) -> p k o", o=128)

                def proj_fm(q, M=128):
                    wt = wchunk(q)
                    bk = gbank()
                    for kc in range(8):
                        MM(bk[0:M, :], wt[:, kc, 0:M], xn[:, kc, :], kc == 0, kc == 7)
                    return bk

                for a in range(2):
                    bk = proj_fm(a)
                    ACT(cqf[:, a, :], bk, AF.Copy)
                    TT("gpsimd", sq2[:, a, :], cqf[:, a, :], cqf[:, a, :], ALU.mult)
                bk = proj_fm(2)
                ACT(ckvf, bk, AF.Copy)
                TT("gpsimd", sqk, ckvf, ckvf, ALU.mult)
                bk = proj_fm(3, M=64)
                TT("vector", krt1[0:32, :], bk[0:32, :], cs[0:32, 0, :], ALU.mult)
                ACT(krA[0:32, :], bk[32:64, :], AF.Copy)
                TT("gpsimd", krA[0:32, :], krA[0:32, :], cs[0:32, 1, :], ALU.mult)
                TT("gpsimd", krb[0:32, :], krt1[0:32, :], krA[0:32, :], ALU.add)
                for a in range(2):
                    bk = proj_fm(4 + a)
                    ACT(xh[:, a, :], bk, AF.Identity, scale=0.5)
                    gelu_half(xh[:, a, :], gt[:, a, :], gu[:, a, :])
                wv0 = wchunk(6)
                wv1 = wchunk(7)
                for tcp in range(2):
                    bk = gbank().rearrange("p (c f) -> p c f", f=256)
                    for tcl in range(2):
                        tc = 2 * tcp + tcl
                        for a, wv in ((0, wv0), (1, wv1)):
                            for kc in range(8):
                                MM(bk[:, tcl, a * 128:(a + 1) * 128], xn[:, kc, tc * 128:(tc + 1) * 128],
                                   wv[:, kc, :], kc == 0, kc == 7)
                    ACT(V1[:, 2 * tcp:2 * tcp + 2, :], bk, AF.Identity, scale=0.5)
                    gelu_half(V1[:, 2 * tcp:2 * tcp + 2, :], V2[:, 2 * tcp:2 * tcp + 2, :],
                              V3[:, 2 * tcp:2 * tcp + 2, :])
                for a in range(2):
                    bk = proj_fm(8 + a)
                    ACT(cvc[:, a, :], bk, AF.Copy)
                for a in range(2):
                    bk = proj_fm(10 + a)
                    TT("vector", zt[:, a, 2:514], bk, cvc[:, a, :], ALU.mult)
                for a in range(2):
                    TS("vector", cvc[:, a, :], zt[:, a, 2:514], P(43 + 4 + a), ALU.mult)
                    STT("vector", cvc[:, a, :], zt[:, a, 1:513], P(43 + 2 + a), cvc[:, a, :], ALU.mult, ALU.add)
                    STT("gpsimd", cvc[:, a, :], zt[:, a, 0:512], P(43 + a), cvc[:, a, :], ALU.mult, ALU.add)
                    bk = proj_fm(12 + a)
                    TT("vector", ycv[:, a, :], bk, cvc[:, a, :], ALU.mult)
                CP("gpsimd", zh[:, l, :, :], zt[:, :, 512:514])
                for a in range(2):
                    bk = proj_fm(14 + a)
                    ACT(p0[:, a, 15:527], bk, AF.Copy)
                TT("gpsimd", pA[:, :, 1:527], p0[:, :, 1:527], p0[:, :, 0:526], ALU.add)
                TT("gpsimd", pB[:, :, 3:527], pA[:, :, 3:527], pA[:, :, 1:525], ALU.add)
                TT("gpsimd", pA[:, 1, 7:527], pB[:, 1, 7:527], pB[:, 1, 3:523], ALU.add)
                TT("gpsimd", pB[:, 1, 15:527], pA[:, 1, 15:527], pA[:, 1, 7:519], ALU.add)
                for a in range(2):
                    for e2 in range(2):
                        rows = slice(e2 * 64, (e2 + 1) * 64)
                        src = (pA if e2 == 0 else pB)
                        win = (2, 4, 8, 16)[2 * a + e2]
                        STT("vector", pooled[rows, a, :], src[rows, a, 15:527], 1.0 / win,
                            p0[rows, a, 15:527], ALU.mult, ALU.subtract)
                        if i == 0:
                            t16 = vst[rows, 0:2, :].rearrange("p a b -> p (a b)")
                            TT("vector", t16, src[rows, a, 15:31], icn[rows, a, :], ALU.mult)
                            TT("vector", pooled[rows, a, 0:16], t16, p0[rows, a, 15:31], ALU.subtract)
                CP("gpsimd", ph[:, l, :, :], p0[:, :, 512:527])
                for a in range(2):
                    bk = gbank()
                    MM(bk, mwt[:, 2048 + a * 128:2048 + (a + 1) * 128], pooled[:, a, :], True, True)
                    TS("vector", ypl[:, a, :], bk, P(49 + a), ALU.mult)
                mu = vst[:, 0, 0:4]
                var = vst[:, 1, 0:4]
                rsd = vst[:, 2, 0:4]
                S.op("vector", lambda e: e.tensor_reduce(out=mu, in_=V3[:, :, :], axis=AX.X, op=ALU.add),
                     reads=[V3[:, :, :]], writes=[mu])
                TS("vector", mu, mu, 1.0 / 256.0, ALU.mult)
                for tc in range(4):
                    TS("vector", V1[:, tc, :], V3[:, tc, :], mu[:, tc:tc + 1], ALU.subtract)
                TT("gpsimd", V2[:, :, :], V1[:, :, :], V1[:, :, :], ALU.mult)
                S.op("vector", lambda e: e.tensor_reduce(out=var, in_=V2[:, :, :], axis=AX.X, op=ALU.add),
                     reads=[V2[:, :, :]], writes=[var])
                bq = gbank()
                MM(bq, ones[:], sq2[:, 0, :], True, False)
                MM(bq, ones[:], sq2[:, 1, :], False, True)
                bkv = gbank()
                MM(bkv, ones[:], sqk, True, True)
                ACT(Rt[:, 0, :], bq, AF.Sqrt, scale=1.0 / 256.0, bias=epsA[:])
                ACT(Rt[:, 2, :], bkv, AF.Sqrt, scale=1.0 / 128.0, bias=epsA[:])
                ACT(rsd, var, AF.Sqrt, scale=1.0 / 256.0, bias=epsA[:])
                RECIP(Rt[:, 1, :], Rt[:, 0, :])
                RECIP(Rt[:, 3, :], Rt[:, 2, :])
                RECIP(rsd, rsd)
                for a in range(2):
                    STT("vector", qn[:, a, :], cqf[:, a, :], P(40 + a), Rt[:, 1, :], ALU.mult, ALU.mult)
                STT("vector", ckvn, ckvf, P(42), Rt[:, 3, :], ALU.mult, ALU.mult)
                for tc in range(4):
                    STT("vector", V2[:, tc, :], V1[:, tc, :], rsd[:, tc:tc + 1], lnp[:, 0:256], ALU.mult, ALU.mult)
                    for e2 in range(2):
                        o_ = vnA[:, tc, :, :].rearrange("p (a e) o -> p a e o", e=2)[:, :, e2, e2 * 64:(e2 + 1) * 64]
                        i0 = V2[:, tc, :].rearrange("p (a e c) -> p a e c", e=2, c=64)[:, :, e2, :]
                        i1 = lnp[:, 256:512].rearrange("p (a e c) -> p a e c", e=2, c=64)[:, :, e2, :]
                        TT("gpsimd", o_, i0, i1, ALU.add)
                for a in range(2):
                    bk = gbank()
                    for tc in range(4):
                        for e2 in range(2):
                            g = 2 * a + e2
                            MM(bk[:, tc * 128:(tc + 1) * 128], vnA[:, tc, g, :],
                               mwt[:, 1536 + g * 128:1536 + (g + 1) * 128], e2 == 0, e2 == 1)
                    TT("vector", sgt.rearrange("p (c t) -> p c t", t=128), bk.rearrange("p (c t) -> p c t", t=128),
                       lnp[:, 512 + a * 128:512 + (a + 1) * 128].unsqueeze(1).broadcast_to([128, 4, 128]), ALU.add)
                    TT("gpsimd", ysg[:, a, :], sgt, gu[:, a, :], ALU.mult)
                DBG("xn", xn[:], i, l)
                DBG("ycv", ycv, i, l)
                DBG("ypl", ypl, i, l)
                DBG("pooled", pooled, i, l)
                DBG("ysg", ysg, i, l)
                DBG("gu", gu, i, l)
                DBG("vnA", vnA[:], i, l)
                DBG("qn", qn, i, l)
                DBG("ckvn", ckvn, i, l)
                qb = []
                for j in range(4):
                    bk = ps[:, [0, 1, 2, 6][j], :]
                    for kc in range(2):
                        MM(bk, mwt[:, j * 256 + kc * 128:j * 256 + (kc + 1) * 128], qn[:, kc, :], kc == 0, kc == 1)
                    qb.append(bk)
                for jn in range(2):
                    for e2 in range(2):
                        CP("scalar" if e2 == 0 else "vector", qT[0:64, 2 * jn + e2, :], qb[jn][e2 * 64:(e2 + 1) * 64, :])
                TT("vector", qr1, qb[2], cs[:, 0, :], ALU.mult)
                TT("vector", qr2, qb[3], cs[:, 1, :], ALU.mult)
                TT("gpsimd", qrb, qr1, qr2, ALU.add)
                for h in range(4):
                    CP("scalar" if h % 2 == 0 else "vector", qT[64:96, h, :], qrb[h * 32:(h + 1) * 32, :])
                for a in range(2):
                    bk = gbank()
                    MM(bk, mwt[:, 1024 + a * 128:1024 + (a + 1) * 128], ckvn, True, True)
                    ACT(knb[:, a, :], bk, AF.Copy)
                for tcp in range(2):
                    bk = gbank().rearrange("p (c f) -> p c f", f=256)
                    for tcl in range(2):
                        tc = 2 * tcp + tcl
                        MM(bk[:, tcl, :], ckvn[:, tc * 128:(tc + 1) * 128], mwt[:, 1280:1536], True, True)
                    for tcl in range(2):
                        tc = 2 * tcp + tcl
                        CP("vector", vsb[:, :, tc, :], bk[:, tcl, :].rearrange("p (h d) -> p h d", d=64))
                for h in range(4):
                    r0 = (l * 4 + h) * 96
                    STORE("sync", kscr[r0:r0 + 64, blk], knb[(h % 2) * 64:(h % 2 + 1) * 64, h // 2, :], st_slot(),
                          dkey=("k", l, h, i))
                    STORE("sync", kscr[r0 + 64:r0 + 96, blk], krb[0:32, :], st_slot(), dkey=("k", l, h, i))
                vr0 = (l * NBLK + i) * 128
                STORE("sync", vscr[vr0:vr0 + 128, :], vsb.rearrange("p h c d -> p (h c d)"), st_slot(),
                      dkey=("v", l, i))
                DBG("qT", qT[0:96], i, l)
                DBG("knb", knb, i, l)
                DBG("krb", krb[0:32], i, l)
                DBG("vsb", vsb, i, l)
                for h in range(4):
                    ob = ps[:, o_banks[h % 2], :]
                    r0 = (l * 4 + h) * 96
                    for j in range(i + 1):
                        sl_ = nxt("kv", NKV)
                        LOAD("sync", kbuf[0:96, sl_, :], kscr[r0:r0 + 96, j * NB:(j + 1) * NB], "kl%d" % sl_,
                             dkey=("k", l, h, j))
                        vr = (l * NBLK + j) * 128
                        LOAD("sync", vbuf[:, sl_, :, 0:64],
                             vscr[vr:vr + 128, h * 256:(h + 1) * 256].rearrange("p (c d) -> p c d", d=64),
                             "vl%d" % sl_, dkey=("v", l, j))
                        for c in range(4):
                            q0 = c * 128 if j == i else 0
                            sbk = ps[:, s_banks[nxt("s", len(s_banks))], :]
                            MM(sbk[:, q0:512], kbuf[0:96, sl_, c * 128:(c + 1) * 128], qT[0:96, h, q0:512], True, True)
                            pt = ptile[:, nxt("pt", NPT), :]
                            ACT(pt[:, q0:512], sbk[:, q0:512], AF.Exp, scale=ATT_SCALE)
                            if j == i:
                                TT("gpsimd", pt[:, q0:q0 + 128], pt[:, q0:q0 + 128], tri[:], ALU.mult)
                            MM(ob[:, q0:512], vbuf[:, sl_, c, :], pt[:, q0:512], (j == 0 and c == 0),
                               (j == i and c == 3))
                    RECIP(rden[0:64, :], ob[64:128, :])
                    TT("vector", oT[0:64, h, :], ob[0:64, :], rden[0:64, :], ALU.mult)
                DBG("oT", oT[0:64], i, l)
                for bi in range(4):
                    brw = W(i, l, 4 + 3 * bi)
                    for half in range(2):
                        gw = W(i, l, 4 + 3 * bi + 1 + half, keep=1 + half)
                        for m4 in range(4):
                            m = half * 4 + m4
                            gbk = gbank()
                            for kc in range(8):
                                MM(gbk, gw[:, m4 * 1024 + kc * 128:m4 * 1024 + (kc + 1) * 128], xn[:, kc, :],
                                   kc == 0, kc == 7)
                            ybk = gbank()
                            if bi == 0:
                                for h in range(4):
                                    MM(ybk, brw[0:64, h * 1024 + m * 128:h * 1024 + (m + 1) * 128], oT[0:64, h, :],
                                       h == 0, h == 3)
                            else:
                                bin_ = (None, ysg, ycv, ypl)[bi]
                                for kc in range(2):
                                    MM(ybk, brw[:, kc * 1024 + m * 128:kc * 1024 + (m + 1) * 128], bin_[:, kc, :],
                                       kc == 0, kc == 1)
                            th = tht[:, nxt("th", 2), :]
                            ACT(th, gbk, AF.Tanh, scale=0.5, bias=hgb[:, l, bi * 8 + m:bi * 8 + m + 1])
                            if bi == 0:
                                STT("vector", macc[:, m, :], th, 1.0, ybk, ALU.add, ALU.mult)
                            else:
                                tmp = gtmp[:, nxt("gt", 2), :]
                                STT("vector", tmp, th, 1.0, ybk, ALU.add, ALU.mult)
                                if bi < 3:
                                    TT("gpsimd", macc[:, m, :], macc[:, m, :], tmp, ALU.add)
                                else:
                                    TT("gpsimd", mbf[:, m, :], macc[:, m, :], tmp, ALU.add)
                DBG("mbf", mbf, i, l)
                for m in range(8):
                    wo = W(i, l, 16 + m // 4)
                    bk = gbank()
                    for kc in range(8):
                        MM(bk, wo[:, (m % 4) * 1024 + kc * 128:(m % 4) * 1024 + (kc + 1) * 128], mbf[:, kc, :],
                           kc == 0, kc == 7)
                    ACT(zf[:, m, :], bk, AF.Copy)
                    TT("gpsimd", sq[:, m, :], zf[:, m, :], zf[:, m, :], ALU.mult)
                post_norm_residual(l, 51, eps4)
                DBG("xmid", xres[:], i, l)
                rms_to_bf16(xres, 59, l, xn, epsA)
                for m in range(32):
                    w1 = W(i, l, 18 + m // 4)
                    bk = gbank()
                    for kc in range(8):
                        MM(bk, w1[:, (m % 4) * 1024 + kc * 128:(m % 4) * 1024 + (kc + 1) * 128], xn[:, kc, :],
                           kc == 0, kc == 7)
                    r_ = rl[:, nxt("rl", 2), :]
                    ACT(r_, bk, AF.Relu)
                    TT("gpsimd", hid[:, m, :], r_, r_, ALU.mult)
                for m in range(8):
                    w2 = W(i, l, 26 + m)
                    bk = gbank()
                    for kc in range(32):
                        MM(bk, w2[:, kc * 128:(kc + 1) * 128], hid[:, kc, :], kc == 0, kc == 31)
                    ACT(zf[:, m, :], bk, AF.Copy)
                    TT("gpsimd", sq[:, m, :], zf[:, m, :], zf[:, m, :], ALU.mult)
                post_norm_residual(l, 67, epsA)
            STORE("sync", outT.rearrange("(c p) t -> p c t", p=128)[:, :, blk], xres[:], "os")
        S.emit(final_wait_slots=["dbg0", "dbg1", "os"] + ["st%d" % k for k in range(NSTS)] + ["ws0", "ws1"])
    return nc


def _chunk_tile(Wm, cols):
    Kd = Wm.shape[0]
    kc = Kd // 128
    t = np.zeros((128, kc, 128), np.float32)
    sub = Wm[:, cols]
    t[:, :, :len(cols)] = sub.reshape(kc, 128, len(cols)).transpose(1, 0, 2)
    return t.reshape(128, kc * 128)


def _prep_weights(inp, L):
    w_in = inp["w_in"]
    wsl = np.zeros((L, NSLOT, 128, SLOTW), np.float32)
    misc = np.zeros((L, 128, NMISC), np.float32)
    par = np.zeros((L, 128, NPAR), np.float32)
    ar = np.arange
    for l in range(L):
        Wi = w_in[l]
        kr = list(range(384, 416)) + list(range(400, 416)) + list(range(384, 400))
        chunks = [ar(0, 128), ar(128, 256), ar(256, 384), np.array(kr),
                  ar(416, 544), ar(544, 672), ar(672, 800), ar(800, 928),
                  ar(1440, 1568), ar(1568, 1696),
                  ar(928, 1056), ar(1056, 1184),
                  ar(1184, 1312), ar(1312, 1440),
                  ar(1696, 1824), ar(1824, 1952)]
        for q, cols in enumerate(chunks):
            wsl[l, q // 4, :, (q % 4) * 1024:(q % 4 + 1) * 1024] = _chunk_tile(Wi, cols)
        brs = [inp["w_br_mla"][l], inp["w_br_sg"][l], inp["w_br_conv"][l], inp["w_br_pool"][l]]
        for bi in range(4):
            s = 4 + 3 * bi
            if bi == 0:
                wsl[l, s, 0:64, :] = brs[0].reshape(4, 64, 1024).transpose(1, 0, 2).reshape(64, 4096)
            else:
                wsl[l, s, :, 0:2048] = brs[bi].reshape(2, 128, 1024).transpose(1, 0, 2).reshape(128, 2048)
            for half in range(2):
                for m4 in range(4):
                    m = half * 4 + m4
                    c0 = 1952 + bi * 1024 + m * 128
                    wsl[l, s + 1 + half, :, m4 * 1024:(m4 + 1) * 1024] = _chunk_tile(Wi, ar(c0, c0 + 128))
        for m in range(8):
            wsl[l, 16 + m // 4, :, (m % 4) * 1024:(m % 4 + 1) * 1024] = _chunk_tile(inp["w_out"][l], ar(m * 128, (m + 1) * 128))
        for m in range(32):
            wsl[l, 18 + m // 4, :, (m % 4) * 1024:(m % 4 + 1) * 1024] = _chunk_tile(inp["w_ff1"][l], ar(m * 128, (m + 1) * 128))
        for m in range(8):
            wsl[l, 26 + m, :, :] = _chunk_tile(inp["w_ff2"][l], ar(m * 128, (m + 1) * 128))
        uq = inp["w_uq"][l]
        n1 = list(range(0, 64)) + list(range(96, 160))
        n2 = list(range(192, 256)) + list(range(288, 352))
        r1 = sum([list(range(h * 96 + 64, h * 96 + 96)) for h in range(4)], [])
        r2 = sum([list(range(h * 96 + 80, h * 96 + 96)) + list(range(h * 96 + 64, h * 96 + 80)) for h in range(4)], [])
        for j, cols in enumerate((n1, n2, r1, r2)):
            misc[l, :, j * 256:(j + 1) * 256] = _chunk_tile(uq, np.array(cols))
        ukv = inp["w_ukv"][l]
        for a in range(2):
            cols = list(range((2 * a) * 128, (2 * a) * 128 + 64)) + list(range((2 * a + 1) * 128, (2 * a + 1) * 128 + 64))
            misc[l, :, 1024 + a * 128:1024 + (a + 1) * 128] = ukv[:, cols]
        vcols = sum([list(range(h * 128 + 64, h * 128 + 128)) for h in range(4)], [])
        misc[l, :, 1280:1536] = ukv[:, vcols]
        for g in range(4):
            misc[l, :, 1536 + g * 128:1536 + (g + 1) * 128] = inp["sg_w"][l, g].T
        for a in range(2):
            blkd = np.zeros((128, 128), np.float32)
            blkd[0:64, 0:64] = inp["pool_w"][l, 2 * a]
            blkd[64:128, 64:128] = inp["pool_w"][l, 2 * a + 1]
            misc[l, :, 2048 + a * 128:2048 + (a + 1) * 128] = blkd
        col = lambda v: v.reshape(-1, 128).T
        par[l, :, 0:8] = col(inp["norm_mix_pre"][l])
        par[l, :, 8:40] = col(inp["gate_b"][l])
        par[l, :, 40:42] = col(inp["q_norm"][l])
        par[l, :, 42:43] = col(inp["kv_norm"][l])
        for k in range(3):
            par[l, :, 43 + 2 * k:45 + 2 * k] = col(inp["conv_w"][l, k])
        par[l, :, 49:51] = col(inp["pool_scale"][l])
        par[l, :, 51:59] = col(inp["norm_mix_post"][l])
        par[l, :, 59:67] = col(inp["norm_ffn_pre"][l])
        par[l, :, 67:75] = col(inp["norm_ffn_post"][l])
    lnp = np.zeros((L, 128, 768), np.float32)
    for l in range(L):
        lnp[l, :, 0:256] = inp["sg_ln_g"][l][None, :]
        lnp[l, :, 256:512] = inp["sg_ln_b"][l][None, :]
        for a in range(2):
            lnp[l, 0:64, 512 + a * 128:512 + (a + 1) * 128] = inp["sg_b"][l, 2 * a][None, :]
            lnp[l, 64:128, 512 + a * 128:512 + (a + 1) * 128] = inp["sg_b"][l, 2 * a + 1][None, :]
    cst = np.zeros((128, 8), np.float32)
    rows = np.arange(128)
    inv_freq = (10000.0 ** (-(np.arange(0, 32, 2, dtype=np.float32)) / 32.0)).astype(np.float32)
    cst[:, 0] = inv_freq[rows % 16]
    cst[:, 1] = np.where((rows % 32) < 16, -1.0, 1.0)
    cst[:, 2] = np.where(rows < 64, 2.0, 4.0)
    cst[:, 3] = np.where(rows < 64, 8.0, 16.0)
    return dict(
        wsl=wsl.reshape(L * NSLOT * 128, SLOTW), misc_in=misc.reshape(L * 128, NMISC),
        par_in=par.reshape(L * 128, NPAR),
        lnp_in=lnp.reshape(L * 128, 768), cst_in=cst)


_PROG_CACHE = {}


def run_module(inputs, T, L, ncores, ret_all=False):
    inp = {k: np.asarray(v) for k, v in inputs.items()}
    wts = _prep_weights(inp, L)
    key = (T, L)
    if key not in _PROG_CACHE:
        _PROG_CACHE[key] = build_program(T, L)
    nc = _PROG_CACHE[key]
    in_maps = []
    for b in range(ncores):
        m = dict(wts)
        m["xT"] = np.ascontiguousarray(inp["x"][b, :T].T.astype(np.float32))
        m["pos"] = np.ascontiguousarray(inp["positions"][b, :T].reshape(1, T).astype(np.int32))
        in_maps.append(m)
    res = run_bass_kernel_spmd(nc, in_maps, core_ids=list(range(ncores)))
    out = np.stack([np.ascontiguousarray(r["outT"].T) for r in res.results], 0)
    if ret_all:
        return out.astype(np.float32), res.results
    return out.astype(np.float32)


def kernel(**inputs):
    return run_module(inputs, SEQ, DEPTH, BATCH)
```

```python
import contextlib
import math
import numpy as np
import concourse.bass as bass
import concourse.mybir as mybir
from concourse.bass_utils import run_bass_kernel_spmd

F32 = mybir.dt.float32
BF16 = mybir.dt.bfloat16
I32 = mybir.dt.int32
AF = mybir.ActivationFunctionType
ALU = mybir.AluOpType
AX = mybir.AxisListType

D_MODEL = 1024
DEPTH = 4
SEQ = 8192
BATCH = 4
NB = 512
EPS = 1e-6
NSLOT = 34
SLOTW = 4096
NPAR = 80
NMISC = 2304
ATT_SCALE = 96.0 ** -0.5
GC1 = 0.7978845608028654
GC2 = 0.044715
TWO_PI = 2.0 * math.pi
CW1 = 6.28125
CW2 = TWO_PI - CW1
PI_SAFE = 3.1415925

ENGS = ["tensor", "vector", "scalar", "gpsimd", "sync"]
PAGE = 512


class _Op:
    __slots__ = ("eng", "fn", "deps", "is_dma", "slot", "seq", "sig", "users")

    def __init__(self, eng, fn, slot, seq):
        self.eng = eng
        self.fn = fn
        self.deps = []
        self.is_dma = slot is not None
        self.slot = slot
        self.seq = seq
        self.sig = None
        self.users = 0


def _keys_of(x):
    if not isinstance(x, bass.AP):
        return [x]
    ap = x.ap
    esz = 2 if x.dtype == BF16 else 4
    pstride = ap[0][0]
    off = x.offset
    fo = off % pstride if pstride > 0 else off
    ext = 0
    for st, cnt in ap[1:]:
        ext += abs(st) * (cnt - 1)
    b0 = fo * esz
    b1 = (fo + ext + 1) * esz - 1
    name = x.tensor.name
    return [(name, p) for p in range(b0 // PAGE, b1 // PAGE + 1)]


class Sched:
    def __init__(self, nc):
        self.nc = nc
        self.streams = {e: [] for e in ENGS}
        self.last_w = {}
        self.readers = {}
        self.last_dma = {}
        self.seq = 0

    def op(self, eng, fn, reads=(), writes=(), dma=None):
        self.seq += 1
        o = _Op(eng, fn, dma, self.seq)
        cand = {}

        def add(d):
            if d is None:
                return
            k = ("s", d.slot) if d.is_dma else ("e", d.eng)
            c = cand.get(k)
            if c is None or c.seq < d.seq:
                cand[k] = d

        rk = []
        for r in reads:
            rk.extend(_keys_of(r))
        wk = []
        for w in writes:
            wk.extend(_keys_of(w))
        for k in rk:
            add(self.last_w.get(k))
        for k in wk:
            add(self.last_w.get(k))
            rd = self.readers.get(k)
            if rd:
                for r in rd.values():
                    add(r)
        if dma is not None:
            add(self.last_dma.get(dma))
            self.last_dma[dma] = o
        for d in cand.values():
            if eng == "tensor" and (not d.is_dma) and d.eng == "tensor":
                continue
            o.deps.append(d)
            d.users += 1
        me = ("s", dma) if dma is not None else ("e", eng)
        for k in rk:
            rd = self.readers.get(k)
            if rd is None:
                rd = {}
                self.readers[k] = rd
            rd[me] = o
        for k in wk:
            self.last_w[k] = o
            self.readers[k] = {}
        self.streams[eng].append(o)
        return o

    def emit(self, final_wait_slots=()):
        nc = self.nc
        eng_cnt = {e: 0 for e in ENGS}
        slot_cnt = {}
        allops = []
        for e in ENGS:
            allops.extend(self.streams[e])
        allops.sort(key=lambda o: o.seq)
        for o in allops:
            if o.is_dma:
                slot_cnt[o.slot] = slot_cnt.get(o.slot, 0) + 16
                o.sig = slot_cnt[o.slot]
            elif o.users > 0:
                eng_cnt[o.eng] += 1
                o.sig = eng_cnt[o.eng]
        with contextlib.ExitStack() as st:
            esem = {e: st.enter_context(nc.semaphore("c_" + e)) for e in ENGS}
            ssem = {s: st.enter_context(nc.semaphore("d_" + str(s))) for s in slot_cnt}
            block = st.enter_context(nc.Block())

            def run_stream(e, engobj):
                known = {}
                for o in self.streams[e]:
                    need = {}
                    for d in o.deps:
                        key = ("s", d.slot) if d.is_dma else ("e", d.eng)
                        if need.get(key, 0) < d.sig:
                            need[key] = d.sig
                    for key, v in need.items():
                        if known.get(key, 0) >= v:
                            continue
                        known[key] = v
                        sem = ssem[key[1]] if key[0] == "s" else esem[key[1]]
                        engobj.wait_ge(sem, v)
                    ins = o.fn(engobj)
                    if o.is_dma:
                        ins.then_inc(ssem[o.slot], 16)
                    elif o.sig is not None:
                        ins.then_inc(esem[e], 1)
                if e == "sync":
                    for s in final_wait_slots:
                        if s in slot_cnt:
                            engobj.wait_ge(ssem[s], slot_cnt[s])

            @block.tensor
            def _(eng):
                run_stream("tensor", eng)

            @block.vector
            def _(eng):
                run_stream("vector", eng)

            @block.scalar
            def _(eng):
                run_stream("scalar", eng)

            @block.gpsimd
            def _(eng):
                run_stream("gpsimd", eng)

            @block.sync
            def _(eng):
                run_stream("sync", eng)


def build_program(T, L, dbg=None):
    NBLK = T // NB
    nc = bass.Bass("TRN2", target_bir_lowering=False)
    dt_in = lambda n, s, d: nc.dram_tensor(n, s, d, kind="ExternalInput").ap()
    xT = dt_in("xT", [D_MODEL, T], F32)
    pos = dt_in("pos", [1, T], I32)
    wsl = dt_in("wsl", [L * NSLOT * 128, SLOTW], F32)
    par_d = dt_in("par_in", [L * 128, NPAR], F32)
    misc_d = dt_in("misc_in", [L * 128, NMISC], F32)
    lnp_d = dt_in("lnp_in", [L * 128, 768], F32)
    cst_d = dt_in("cst_in", [128, 8], F32)
    outT = nc.dram_tensor("outT", [D_MODEL, T], F32, kind="ExternalOutput").ap()
    wscr = nc.dram_tensor("wscr", [L * NSLOT * 128, SLOTW], BF16, kind="Internal").ap()
    kscr = nc.dram_tensor("kscr", [L * 4 * 96, T], BF16, kind="Internal").ap()
    vscr = nc.dram_tensor("vscr", [L * NBLK * 128, 1024], BF16, kind="Internal").ap()
    csscr = nc.dram_tensor("csscr", [128, 2 * T], F32, kind="Internal").ap()
    mscr = nc.dram_tensor("mscr", [L * 128, NMISC], BF16, kind="Internal").ap()

    S = Sched(nc)
    st = contextlib.ExitStack()
    with st:
        def sb(name, shape, dt):
            return st.enter_context(nc.sbuf_tensor(name, shape, dt))

        ps = st.enter_context(nc.psum_tensor("ps", [128, 8, 512], F32))

        xres = sb("xres", [128, 8, 512], F32)
        xn = sb("xn", [128, 8, 512], BF16)
        sq = sb("sq", [128, 8, 512], BF16)
        NRING = 5
        wring = sb("wring", [128, NRING, SLOTW], BF16)
        cs = sb("cs", [128, 2, 512], F32)
        par = sb("par", [128, L, NPAR], F32)
        hgb = sb("hgb", [128, L, 32], F32)
        mwt = sb("mwt", [128, NMISC], BF16)
        lnp = sb("lnp", [128, 768], F32)
        cst = sb("cst", [128, 8], F32)
        ones = sb("ones", [128, 128], BF16)
        tri = sb("tri", [128, 128], BF16)
        epsA = sb("epsA", [128, 1], F32)
        eps4 = sb("eps4", [128, 1], F32)
        zh = sb("zh", [128, L, 2, 2], F32)
        ph = sb("ph", [128, L, 2, 15], F32)
        icn = sb("icn", [128, 2, 16], F32)
        Rt = sb("Rt", [128, 4, 512], F32)
        FA = sb("FA", [128, 16960], F32)
        BA = sb("BA", [128, 17 * 1024], BF16)

        def fa(off_k, shape):
            n = int(np.prod(shape))
            v = FA[:, off_k:off_k + n]
            if len(shape) == 1:
                return v
            if len(shape) == 2:
                return v.rearrange("p (a b) -> p a b", b=shape[1])
            return v.rearrange("p (a b c) -> p a b c", b=shape[1], c=shape[2])

        def ba(off_k, shape):
            n = int(np.prod(shape))
            v = BA[:, off_k:off_k + n]
            if len(shape) == 1:
                return v
            if len(shape) == 2:
                return v.rearrange("p (a b) -> p a b", b=shape[1])
            return v.rearrange("p (a b c) -> p a b c", b=shape[1], c=shape[2])

        K = 1024
        macc = fa(0, [8, 512])
        zf = macc
        tht = fa(4 * K, [2, 512])
        gtmp = fa(5 * K, [2, 512])
        rl = fa(6 * K, [2, 512])
        cqf = fa(0, [2, 512])
        ckvf = fa(1 * K, [512])
        krt1 = fa(1 * K + 512, [512])
        krA = fa(2 * K, [512])
        qr1 = fa(2 * K + 512, [512])
        qr2 = fa(3 * K, [512])
        qrb = fa(3 * K + 512, [512])
        xh = fa(4 * K, [2, 512])
        gu = fa(5 * K, [2, 512])
        gt = fa(6 * K, [2, 512])
        V1 = fa(7 * K, [4, 256])
        V2 = fa(8 * K, [4, 256])
        V3 = fa(9 * K, [4, 256])
        cvc = fa(10 * K, [2, 512])
        zt = fa(11 * K, [2, 514])
        p0 = fa(12 * K + 8, [2, 527])
        pA = fa(13 * K + 64, [2, 527])
        pB = fa(14 * K + 128, [2, 527])
        sgt = fa(15 * K + 256, [512])
        rden = fa(16 * K, [512])
        vst = fa(16 * K + 512, [4, 8])
        hid = ba(0, [32, 512])
        mbf = ba(11 * K, [8, 512])
        NKV = 3
        kbuf = ba(0, [NKV, 512])
        NPT = 4
        ptile = ba(4 * K, [NPT, 512])
        qT = ba(6 * K, [4, 512])
        qn = ba(10 * K, [2, 512])
        ckvn = ba(11 * K, [512])
        sqk = ba(11 * K + 512, [512])
        sq2 = ba(12 * K, [2, 512])
        krb = ba(13 * K, [512])
        knb = ba(14 * K, [2, 512])
        vsb = ba(8 * K, [1024]).rearrange("p (h c d) -> p h c d", c=4, d=64)
        oT = ba(15 * K, [4, 512])
        vbuf = sb("vbuf", [128, NKV, 4, 128], BF16)
        vnA = sb("vnA", [128, 4, 4, 128], BF16)
        brin = sb("brin", [128, 6, 512], BF16)
        ysg = brin[:, 0:2, :]
        ycv = brin[:, 2:4, :]
        ypl = brin[:, 4:6, :]
        pooled = ba(9 * K, [2, 512])

        gp_banks = [0, 1, 2, 3, 4, 5]
        s_banks = [3, 4, 5]
        o_banks = [6, 7]
        rot = {"gp": 0, "s": 0, "pt": 0, "kv": 0, "th": 0, "gt": 0, "rl": 0, "st": 0}

        def nxt(name, n):
            v = rot[name]
            rot[name] = (v + 1) % n
            return v

        def gbank():
            return ps[:, gp_banks[nxt("gp", len(gp_banks))], :]

        def MM(out, lhsT, rhs, start, stop):
            rd = [lhsT, rhs] if start else [lhsT, rhs, out]
            S.op("tensor", lambda e: e.matmul(out, lhsT, rhs, start=start, stop=stop),
                 reads=rd, writes=[out])

        def ACT(out, in_, func, scale=1.0, bias=None, extra_reads=()):
            rd = [in_] + list(extra_reads)
            if isinstance(scale, bass.AP):
                rd.append(scale)
            if bias is None:
                S.op("scalar", lambda e: e.activation(out=out, in_=in_, func=func, scale=scale),
                     reads=rd, writes=[out])
            else:
                rd.append(bias)
                S.op("scalar", lambda e: e.activation(out=out, in_=in_, func=func, scale=scale, bias=bias),
                     reads=rd, writes=[out])

        def TT(eng, out, in0, in1, op):
            S.op(eng, lambda e: e.tensor_tensor(out=out, in0=in0, in1=in1, op=op),
                 reads=[in0, in1], writes=[out])

        def TS(eng, out, in0, s1, op0, s2=None, op1=None):
            rd = [in0]
            if isinstance(s1, bass.AP):
                rd.append(s1)
            if isinstance(s2, bass.AP):
                rd.append(s2)
            if op1 is None:
                S.op(eng, lambda e: e.tensor_scalar(out=out, in0=in0, scalar1=s1, scalar2=None, op0=op0),
                     reads=rd, writes=[out])
            else:
                S.op(eng, lambda e: e.tensor_scalar(out=out, in0=in0, scalar1=s1, scalar2=s2, op0=op0, op1=op1),
                     reads=rd, writes=[out])

        def STT(eng, out, in0, scalar, in1, op0, op1):
            eng = "vector"
            rd = [in0, in1]
            if isinstance(scalar, bass.AP):
                rd.append(scalar)
            S.op(eng, lambda e: e.scalar_tensor_tensor(out=out, in0=in0, scalar=scalar, in1=in1, op0=op0, op1=op1),
                 reads=rd, writes=[out])

        def CP(eng, out, in_):
            if eng == "scalar":
                ACT(out, in_, AF.Copy)
            else:
                S.op(eng, lambda e: e.tensor_copy(out=out, in_=in_), reads=[in_], writes=[out])

        def MEMSET(eng, out, val):
            S.op(eng, lambda e: e.memset(out, val), writes=[out])

        def RECIP(out, in_):
            S.op("vector", lambda e: e.reciprocal(out=out, in_=in_), reads=[in_], writes=[out])

        def DMA(eng, out, in_, slot, reads=(), writes=()):
            S.op(eng, lambda e: e.dma_start(out=out, in_=in_), reads=list(reads), writes=list(writes), dma=slot)

        def LOAD(eng, out, in_, slot, dkey=None):
            DMA(eng, out, in_, slot, reads=([dkey] if dkey is not None else []), writes=[out])

        def STORE(eng, out, in_, slot, dkey=None):
            DMA(eng, out, in_, slot, reads=[in_], writes=([dkey] if dkey is not None else []))

        dbg_n = [0]

        def DBG(name, ap, i, l):
            if dbg is None or (i, l) != dbg:
                return
            shp = list(ap.shape)
            d = nc.dram_tensor("dbg_" + name, shp, ap.dtype, kind="ExternalOutput").ap()
            dbg_n[0] += 1
            STORE("sync", d, ap, "dbg%d" % (dbg_n[0] % 2))

        NSTS = 4

        def st_slot():
            return "st%d" % nxt("st", NSTS)

        LOAD("sync", cst[:], cst_d, "su0")
        for l in range(L):
            LOAD("sync", par[:, l, :], par_d[l * 128:(l + 1) * 128, :], "su1")
        MEMSET("vector", ones[:], 1.0)
        MEMSET("vector", epsA[:], EPS)
        MEMSET("vector", eps4[:], 4.0 * EPS)
        MEMSET("vector", zh[:], 0.0)
        MEMSET("vector", ph[:], 0.0)
        MEMSET("gpsimd", BA[:], 0.0)
        MEMSET("gpsimd", FA[:], 0.0)
        trf = Rt[:, 0, 0:128]
        MEMSET("gpsimd", trf, 1.0)
        S.op("gpsimd", lambda e: e.affine_select(out=trf, in_=trf, pattern=[[1, 128]], compare_op=ALU.is_ge,
                                                 fill=0.0, base=0, channel_multiplier=-1),
             reads=[trf], writes=[trf])
        CP("vector", tri[:], trf)
        MEMSET("gpsimd", vnA[:], 0.0)
        MEMSET("gpsimd", vbuf[:], 0.0)
        for s_ in range(NKV):
            MEMSET("gpsimd", vbuf[:, s_, :, 64:128], 1.0)
        for l in range(L):
            TS("vector", hgb[:, l, :], par[:, l, 8:40], 0.5, ALU.mult)
            LOAD("gpsimd", mwt[:], misc_d[l * 128:(l + 1) * 128, :], "su2")
            for g in range(4):
                w_ = mwt[:, 1536 + g * 128:1536 + (g + 1) * 128]
                TT("gpsimd", w_, w_, tri[:], ALU.mult)
            STORE("sync", mscr[l * 128:(l + 1) * 128, :], mwt[:], st_slot(), dkey=("m", l))
        ici = Rt[:, 1, 0:16].bitcast(I32)
        S.op("gpsimd", lambda e: e.iota(ici, pattern=[[1, 16]], base=1, channel_multiplier=0),
             writes=[ici])
        icf = Rt[:, 2, 0:16]
        CP("vector", icf, ici)
        for a in range(2):
            TS("vector", icn[:, a, :], icf, cst[:, 2 + a:3 + a], ALU.min)
            RECIP(icn[:, a, :], icn[:, a, :])
        posi = Rt[:, 0, :].bitcast(I32)
        ang = Rt[:, 1, :]
        kk = Rt[:, 2, :]
        mm_ = Rt[:, 3, :]
        cstile = cs
        for c in range(NBLK):
            sl = slice(c * NB, (c + 1) * NB)
            LOAD("sync", posi, pos[0:1, sl].broadcast_to([128, NB]), "su2")
            CP("vector", ang, posi)
            TS("vector", ang, ang, cst[:, 0:1], ALU.mult)
            TS("vector", kk, ang, 1.0 / TWO_PI, ALU.mult, 12582912.0, ALU.add)
            TS("vector", kk, kk, 12582912.0, ALU.subtract)
            STT("vector", ang, kk, -CW1, ang, ALU.mult, ALU.add)
            STT("vector", ang, kk, -CW2, ang, ALU.mult, ALU.add)
            for sgn in (1.0, -1.0):
                if sgn > 0:
                    TS("vector", mm_, ang, math.pi, ALU.is_gt, -TWO_PI, ALU.mult)
                else:
                    TS("vector", mm_, ang, -math.pi, ALU.is_lt, TWO_PI, ALU.mult)
                TT("vector", ang, ang, mm_, ALU.add)
            TS("vector", kk, ang, PI_SAFE, ALU.min, -PI_SAFE, ALU.max)
            ACT(cstile[:, 1, :], kk, AF.Sin)
            TS("vector", cstile[:, 1, :], cstile[:, 1, :], cst[:, 1:2], ALU.mult)
            TS("vector", kk, ang, math.pi / 2.0, ALU.add)
            TS("vector", mm_, kk, math.pi, ALU.is_gt, -TWO_PI, ALU.mult)
            TT("vector", kk, kk, mm_, ALU.add)
            TS("vector", kk, kk, PI_SAFE, ALU.min, -PI_SAFE, ALU.max)
            ACT(cstile[:, 0, :], kk, AF.Sin)
            STORE("sync", csscr[:, c * NB:(c + 1) * NB], cstile[:, 0, :], st_slot(), dkey=("cs", c))
            STORE("sync", csscr[:, T + c * NB:T + (c + 1) * NB], cstile[:, 1, :], st_slot(), dkey=("cs", c))

        wst = {"next": 0}

        def slot_region(s):
            if s == 4:
                return 64, SLOTW
            if s in (7, 10, 13):
                return 128, 2048
            return 128, SLOTW

        def issue_loads_upto(n):
            while wst["next"] <= n:
                m = wst["next"]
                bi_, rem = divmod(m, L * NSLOT)
                l_, s_ = divmod(rem, NSLOT)
                buf = wring[:, m % NRING, :]
                row0 = (l_ * NSLOT + s_) * 128
                if bi_ == 0:
                    LOAD("gpsimd", buf, wsl[row0:row0 + 128, :], "wl%d" % (m % NRING))
                    STORE("sync", wscr[row0:row0 + 128, :], buf, "ws%d" % (m % 2), dkey=("w", l_, s_))
                else:
                    pp, cc = slot_region(s_)
                    LOAD("sync", wring[0:pp, m % NRING, 0:cc], wscr[row0:row0 + pp, 0:cc],
                         "wl%d" % (m % NRING), dkey=("w", l_, s_))
                wst["next"] += 1

        def W(i, l, s, keep=0):
            n = (i * L + l) * NSLOT + s
            issue_loads_upto(min(n + NRING - 1 - keep, NBLK * L * NSLOT - 1))
            return wring[:, n % NRING, :]

        def rms_to_bf16(src, gcol0, l, dst, eps_t):
            for c in range(8):
                if c % 2 == 0:
                    ACT(sq[:, c, :], src[:, c, :], AF.Square)
                else:
                    TT("gpsimd", sq[:, c, :], src[:, c, :], src[:, c, :], ALU.mult)
            bk = gbank()
            for c in range(8):
                MM(bk, ones[:], sq[:, c, :], c == 0, c == 7)
            ACT(Rt[:, 0, :], bk, AF.Sqrt, scale=1.0 / D_MODEL, bias=eps_t[:])
            RECIP(Rt[:, 1, :], Rt[:, 0, :])
            for c in range(8):
                STT("vector" if c % 2 == 0 else "gpsimd", dst[:, c, :], src[:, c, :],
                    par[:, l, gcol0 + c:gcol0 + c + 1], Rt[:, 1, :], ALU.mult, ALU.mult)

        def post_norm_residual(l, gcol0, eps_t):
            bk = gbank()
            for c in range(8):
                MM(bk, ones[:], sq[:, c, :], c == 0, c == 7)
            ACT(Rt[:, 0, :], bk, AF.Sqrt, scale=1.0 / D_MODEL, bias=eps_t[:])
            RECIP(Rt[:, 1, :], Rt[:, 0, :])
            for c in range(8):
                tmp = gtmp[:, c % 2, :]
                STT("vector", tmp, zf[:, c, :], par[:, l, gcol0 + c:gcol0 + c + 1], Rt[:, 1, :],
                    ALU.mult, ALU.mult)
                TT("gpsimd", xres[:, c, :], xres[:, c, :], tmp, ALU.add)

        def gelu_half(xh_ap, tmp_ap, out_ap):
            ACT(tmp_ap, xh_ap, AF.Square)
            TS("vector", tmp_ap, tmp_ap, 4.0 * GC2, ALU.mult, 1.0, ALU.add)
            TT("gpsimd", tmp_ap, tmp_ap, xh_ap, ALU.mult)
            ACT(tmp_ap, tmp_ap, AF.Tanh, scale=2.0 * GC1)
            STT("vector", out_ap, tmp_ap, 1.0, xh_ap, ALU.add, ALU.mult)

        for i in range(NBLK):
            blk = slice(i * NB, (i + 1) * NB)
            LOAD("sync", xres[:], xT.rearrange("(c p) t -> p c t", p=128)[:, :, blk], "xl")
            LOAD("sync", cs[:, 0, :], csscr[:, i * NB:(i + 1) * NB], "cl0", dkey=("cs", i))
            LOAD("sync", cs[:, 1, :], csscr[:, T + i * NB:T + (i + 1) * NB], "cl1", dkey=("cs", i))
            for l in range(L):
                P = lambda c0: par[:, l, c0:c0 + 1]
                LOAD("sync", mwt[:], mscr[l * 128:(l + 1) * 128, :], "ml", dkey=("m", l))
                LOAD("sync", lnp[:], lnp_d[l * 128:(l + 1) * 128, :], "pl")
                rms_to_bf16(xres, 0, l, xn, epsA)
                CP("gpsimd", zt[:, :, 0:2], zh[:, l, :, :])
                CP("gpsimd", p0[:, :, 0:15], ph[:, l, :, :])
                def wchunk(q):
                    wt = W(i, l, q // 4)
                    return wt[:, (q % 4) * 1024:(q % 4 + 1) * 1024].rearrange("p (k o) -> p k o", o=128)

                def proj_fm(q, M=128):
                    wt = wchunk(q)
                    bk = gbank()
                    for kc in range(8):
                        MM(bk[0:M, :], wt[:, kc, 0:M], xn[:, kc, :], kc == 0, kc == 7)
                    return bk

                for a in range(2):
                    bk = proj_fm(a)
                    ACT(cqf[:, a, :], bk, AF.Copy)
                    TT("gpsimd", sq2[:, a, :], cqf[:, a, :], cqf[:, a, :], ALU.mult)
                bk = proj_fm(2)
                ACT(ckvf, bk, AF.Copy)
                TT("gpsimd", sqk, ckvf, ckvf, ALU.mult)
                bk = proj_fm(3, M=64)
                TT("vector", krt1[0:32, :], bk[0:32, :], cs[0:32, 0, :], ALU.mult)
                ACT(krA[0:32, :], bk[32:64, :], AF.Copy)
                TT("gpsimd", krA[0:32, :], krA[0:32, :], cs[0:32, 1, :], ALU.mult)
                TT("gpsimd", krb[0:32, :], krt1[0:32, :], krA[0:32, :], ALU.add)
                bq = gbank()
                MM(bq, ones[:], sq2[:, 0, :], True, False)
                MM(bq, ones[:], sq2[:, 1, :], False, True)
                bkv = gbank()
                MM(bkv, ones[:], sqk, True, True)
                ACT(Rt[:, 0, :], bq, AF.Sqrt, scale=1.0 / 256.0, bias=epsA[:])
                ACT(Rt[:, 2, :], bkv, AF.Sqrt, scale=1.0 / 128.0, bias=epsA[:])
                RECIP(Rt[:, 1, :], Rt[:, 0, :])
                RECIP(Rt[:, 3, :], Rt[:, 2, :])
                for a in range(2):
                    STT("vector", qn[:, a, :], cqf[:, a, :], P(40 + a), Rt[:, 1, :], ALU.mult, ALU.mult)
                STT("vector", ckvn, ckvf, P(42), Rt[:, 3, :], ALU.mult, ALU.mult)
                for a in range(2):
                    bk = proj_fm(4 + a)
                    ACT(xh[:, a, :], bk, AF.Identity, scale=0.5)
                    gelu_half(xh[:, a, :], gt[:, a, :], gu[:, a, :])
                wv0 = wchunk(6)
                wv1 = wchunk(7)
                for tcp in range(2):
                    bk = gbank().rearrange("p (c f) -> p c f", f=256)
                    for tcl in range(2):
                        tc = 2 * tcp + tcl
                        for a, wv in ((0, wv0), (1, wv1)):
                            for kc in range(8):
                                MM(bk[:, tcl, a * 128:(a + 1) * 128], xn[:, kc, tc * 128:(tc + 1) * 128],
                                   wv[:, kc, :], kc == 0, kc == 7)
                    ACT(V1[:, 2 * tcp:2 * tcp + 2, :], bk, AF.Identity, scale=0.5)
                    gelu_half(V1[:, 2 * tcp:2 * tcp + 2, :], V2[:, 2 * tcp:2 * tcp + 2, :],
                              V3[:, 2 * tcp:2 * tcp + 2, :])
                for a in range(2):
                    bk = proj_fm(8 + a)
                    ACT(cvc[:, a, :], bk, AF.Copy)
                for a in range(2):
                    bk = proj_fm(10 + a)
                    TT("vector", zt[:, a, 2:514], bk, cvc[:, a, :], ALU.mult)
                for a in range(2):
                    TS("vector", cvc[:, a, :], zt[:, a, 2:514], P(43 + 4 + a), ALU.mult)
                    STT("vector", cvc[:, a, :], zt[:, a, 1:513], P(43 + 2 + a), cvc[:, a, :], ALU.mult, ALU.add)
                    STT("gpsimd", cvc[:, a, :], zt[:, a, 0:512], P(43 + a), cvc[:, a, :], ALU.mult, ALU.add)
                    bk = proj_fm(12 + a)
                    TT("vector", ycv[:, a, :], bk, cvc[:, a, :], ALU.mult)
                CP("gpsimd", zh[:, l, :, :], zt[:, :, 512:514])
                for a in range(2):
                    bk = proj_fm(14 + a)
                    ACT(p0[:, a, 15:527], bk, AF.Copy)
                TT("gpsimd", pA[:, :, 1:527], p0[:, :, 1:527], p0[:, :, 0:526], ALU.add)
                TT("gpsimd", pB[:, :, 3:527], pA[:, :, 3:527], pA[:, :, 1:525], ALU.add)
                TT("gpsimd", pA[:, 1, 7:527], pB[:, 1, 7:527], pB[:, 1, 3:523], ALU.add)
                TT("gpsimd", pB[:, 1, 15:527], pA[:, 1, 15:527], pA[:, 1, 7:519], ALU.add)
                for a in range(2):
                    for e2 in range(2):
                        rows = slice(e2 * 64, (e2 + 1) * 64)
                        src = (pA if e2 == 0 else pB)
                        win = (2, 4, 8, 16)[2 * a + e2]
                        STT("vector", pooled[rows, a, :], src[rows, a, 15:527], 1.0 / win,
                            p0[rows, a, 15:527], ALU.mult, ALU.subtract)
                        if i == 0:
                            t16 = vst[rows, 0:2, :].rearrange("p a b -> p (a b)")
                            TT("vector", t16, src[rows, a, 15:31], icn[rows, a, :], ALU.mult)
                            TT("vector", pooled[rows, a, 0:16], t16, p0[rows, a, 15:31], ALU.subtract)
                CP("gpsimd", ph[:, l, :, :], p0[:, :, 512:527])
                def late_pool_mix():
                    for a in range(2):
                        bk = gbank()
                        MM(bk, mwt[:, 2048 + a * 128:2048 + (a + 1) * 128], pooled[:, a, :], True, True)
                        TS("vector", ypl[:, a, :], bk, P(49 + a), ALU.mult)
                def late_sg():
                    for a in range(2):
                        bk = gbank()
                        for tc in range(4):
                            for e2 in range(2):
                                g = 2 * a + e2
                                MM(bk[:, tc * 128:(tc + 1) * 128], vnA[:, tc, g, :],
                                   mwt[:, 1536 + g * 128:1536 + (g + 1) * 128], e2 == 0, e2 == 1)
                        TT("vector", sgt.rearrange("p (c t) -> p c t", t=128), bk.rearrange("p (c t) -> p c t", t=128),
                           lnp[:, 512 + a * 128:512 + (a + 1) * 128].unsqueeze(1).broadcast_to([128, 4, 128]), ALU.add)
                        TT("gpsimd", ysg[:, a, :], sgt, gu[:, a, :], ALU.mult)
                DBG("xn", xn[:], i, l)
                DBG("gu", gu, i, l)
                DBG("vnA", vnA[:], i, l)
                DBG("qn", qn, i, l)
                DBG("ckvn", ckvn, i, l)
                qb = []
                for j in range(4):
                    bk = ps[:, [0, 1, 2, 6][j], :]
                    for kc in range(2):
                        MM(bk, mwt[:, j * 256 + kc * 128:j * 256 + (kc + 1) * 128], qn[:, kc, :], kc == 0, kc == 1)
                    qb.append(bk)
                for jn in range(2):
                    for e2 in range(2):
                        CP("scalar" if e2 == 0 else "vector", qT[0:64, 2 * jn + e2, :], qb[jn][e2 * 64:(e2 + 1) * 64, :])
                TT("vector", qr1, qb[2], cs[:, 0, :], ALU.mult)
                TT("vector", qr2, qb[3], cs[:, 1, :], ALU.mult)
                TT("gpsimd", qrb, qr1, qr2, ALU.add)
                for h in range(4):
                    CP("scalar" if h % 2 == 0 else "vector", qT[64:96, h, :], qrb[h * 32:(h + 1) * 32, :])
                for a in range(2):
                    bk = gbank()
                    MM(bk, mwt[:, 1024 + a * 128:1024 + (a + 1) * 128], ckvn, True, True)
                    ACT(knb[:, a, :], bk, AF.Copy)
                for tcp in range(2):
                    bk = gbank().rearrange("p (c f) -> p c f", f=256)
                    for tcl in range(2):
                        tc = 2 * tcp + tcl
                        MM(bk[:, tcl, :], ckvn[:, tc * 128:(tc + 1) * 128], mwt[:, 1280:1536], True, True)
                    for tcl in range(2):
                        tc = 2 * tcp + tcl
                        CP("vector", vsb[:, :, tc, :], bk[:, tcl, :].rearrange("p (h d) -> p h d", d=64))
                groups = [(h, j) for h in range(4) for j in range(i + 1)]
                gslot = {}
                gnext = [0]

                def issue_group_loads(upto):
                    while gnext[0] <= upto and gnext[0] < len(groups):
                        h_, j_ = groups[gnext[0]]
                        sl2 = nxt("kv", NKV)
                        gslot[(h_, j_)] = sl2
                        r0_ = (l * 4 + h_) * 96
                        LOAD("sync", kbuf[0:96, sl2, :], kscr[r0_:r0_ + 96, j_ * NB:(j_ + 1) * NB], "kl%d" % sl2,
                             dkey=("k", l, h_, j_))
                        vr = (l * NBLK + j_) * 128
                        LOAD("sync", vbuf[:, sl2, :, 0:64],
                             vscr[vr:vr + 128, h_ * 256:(h_ + 1) * 256].rearrange("p (c d) -> p c d", d=64),
                             "vl%d" % sl2, dkey=("v", l, j_))
                        gnext[0] += 1

                npre = 0
                while npre < NKV - 1 and npre < len(groups) and groups[npre][1] < i:
                    npre += 1
                if npre > 0:
                    issue_group_loads(npre - 1)
                for h in range(4):
                    r0 = (l * 4 + h) * 96
                    STORE("sync", kscr[r0:r0 + 64, blk], knb[(h % 2) * 64:(h % 2 + 1) * 64, h // 2, :], st_slot(),
                          dkey=("k", l, h, i))
                    STORE("sync", kscr[r0 + 64:r0 + 96, blk], krb[0:32, :], st_slot(), dkey=("k", l, h, i))
                vr0 = (l * NBLK + i) * 128
                STORE("sync", vscr[vr0:vr0 + 128, :], vsb.rearrange("p h c d -> p (h c d)"), st_slot(),
                      dkey=("v", l, i))
                DBG("qT", qT[0:96], i, l)
                DBG("knb", knb, i, l)
                DBG("krb", krb[0:32], i, l)
                DBG("vsb", vsb, i, l)
                mu = vst[:, 0, 0:4]
                var = vst[:, 1, 0:4]
                rsd = vst[:, 2, 0:4]
                S.op("vector", lambda e: e.tensor_reduce(out=mu, in_=V3[:, :, :], axis=AX.X, op=ALU.add),
                     reads=[V3[:, :, :]], writes=[mu])
                TS("vector", mu, mu, 1.0 / 256.0, ALU.mult)
                for tc in range(4):
                    TS("vector", V1[:, tc, :], V3[:, tc, :], mu[:, tc:tc + 1], ALU.subtract)
                TT("gpsimd", V2[:, :, :], V1[:, :, :], V1[:, :, :], ALU.mult)
                S.op("vector", lambda e: e.tensor_reduce(out=var, in_=V2[:, :, :], axis=AX.X, op=ALU.add),
                     reads=[V2[:, :, :]], writes=[var])
                ACT(rsd, var, AF.Sqrt, scale=1.0 / 256.0, bias=epsA[:])
                RECIP(rsd, rsd)
                for tc in range(4):
                    STT("vector", V2[:, tc, :], V1[:, tc, :], rsd[:, tc:tc + 1], lnp[:, 0:256], ALU.mult, ALU.mult)
                    for e2 in range(2):
                        o_ = vnA[:, tc, :, :].rearrange("p (a e) o -> p a e o", e=2)[:, :, e2, e2 * 64:(e2 + 1) * 64]
                        i0 = V2[:, tc, :].rearrange("p (a e c) -> p a e c", e=2, c=64)[:, :, e2, :]
                        i1 = lnp[:, 256:512].rearrange("p (a e c) -> p a e c", e=2, c=64)[:, :, e2, :]
                        TT("gpsimd", o_, i0, i1, ALU.add)
                steps = [(h, j, c) for h in range(4) for j in range(i + 1) for c in range(4)]

                LA = 2
                pend = {}
                for idx in range(len(steps) + LA):
                    if idx < len(steps):
                        h, j, c = steps[idx]
                        if c == 0:
                            issue_group_loads(groups.index((h, j)) + 1)
                        sl_ = gslot[(h, j)]
                        q0 = c * 128 if j == i else 0
                        sbk = ps[:, s_banks[nxt("s", len(s_banks))], :]
                        MM(sbk[:, q0:512], kbuf[0:96, sl_, c * 128:(c + 1) * 128], qT[0:96, h, q0:512], True, True)
                        pt = ptile[:, nxt("pt", NPT), :]
                        ACT(pt[:, q0:512], sbk[:, q0:512], AF.Exp, scale=ATT_SCALE)
                        if j == i:
                            TT("gpsimd", pt[:, q0:q0 + 128], pt[:, q0:q0 + 128], tri[:], ALU.mult)
                        pend[idx] = (sl_, q0, pt)
                    if idx >= LA:
                        h, j, c = steps[idx - LA]
                        sl_, q0, pt = pend.pop(idx - LA)
                        ob = ps[:, o_banks[h % 2], :]
                        MM(ob[:, q0:512], vbuf[:, sl_, c, :], pt[:, q0:512], (j == 0 and c == 0),
                           (j == i and c == 3))
                        if j == i and c == 3:
                            RECIP(rden[0:64, :], ob[64:128, :])
                            TT("vector", oT[0:64, h, :], ob[0:64, :], rden[0:64, :], ALU.mult)
                late_pool_mix()
                late_sg()
                DBG("oT", oT[0:64], i, l)
                DBG("ycv", ycv, i, l)
                DBG("ypl", ypl, i, l)
                DBG("ysg", ysg, i, l)
                for bi in range(4):
                    brw = W(i, l, 4 + 3 * bi)
                    for half in range(2):
                        gw = W(i, l, 4 + 3 * bi + 1 + half, keep=1 + half)
                        for m4 in range(4):
                            m = half * 4 + m4
                            gbk = gbank()
                            for kc in range(8):
                                MM(gbk, gw[:, m4 * 1024 + kc * 128:m4 * 1024 + (kc + 1) * 128], xn[:, kc, :],
                                   kc == 0, kc == 7)
                            ybk = gbank()
                            if bi == 0:
                                for h in range(4):
                                    MM(ybk, brw[0:64, h * 1024 + m * 128:h * 1024 + (m + 1) * 128], oT[0:64, h, :],
                                       h == 0, h == 3)
                            else:
                                bin_ = (None, ysg, ycv, ypl)[bi]
                                for kc in range(2):
                                    MM(ybk, brw[:, kc * 1024 + m * 128:kc * 1024 + (m + 1) * 128], bin_[:, kc, :],
                                       kc == 0, kc == 1)
                            th = tht[:, nxt("th", 2), :]
                            ACT(th, gbk, AF.Tanh, scale=0.5, bias=hgb[:, l, bi * 8 + m:bi * 8 + m + 1])
                            if bi == 0:
                                STT("vector", macc[:, m, :], th, 1.0, ybk, ALU.add, ALU.mult)
                            else:
                                tmp = gtmp[:, nxt("gt", 2), :]
                                STT("vector", tmp, th, 1.0, ybk, ALU.add, ALU.mult)
                                if bi < 3:
                                    TT("gpsimd", macc[:, m, :], macc[:, m, :], tmp, ALU.add)
                                else:
                                    TT("gpsimd", mbf[:, m, :], macc[:, m, :], tmp, ALU.add)
                DBG("mbf", mbf, i, l)
                for m in range(8):
                    wo = W(i, l, 16 + m // 4)
                    bk = gbank()
                    for kc in range(8):
                        MM(bk, wo[:, (m % 4) * 1024 + kc * 128:(m % 4) * 1024 + (kc + 1) * 128], mbf[:, kc, :],
                           kc == 0, kc == 7)
                    ACT(zf[:, m, :], bk, AF.Copy)
                    TT("gpsimd", sq[:, m, :], zf[:, m, :], zf[:, m, :], ALU.mult)
                post_norm_residual(l, 51, eps4)
                DBG("xmid", xres[:], i, l)
                rms_to_bf16(xres, 59, l, xn, epsA)
                for m in range(32):
                    w1 = W(i, l, 18 + m // 4)
                    bk = gbank()
                    for kc in range(8):
                        MM(bk, w1[:, (m % 4) * 1024 + kc * 128:(m % 4) * 1024 + (kc + 1) * 128], xn[:, kc, :],
                           kc == 0, kc == 7)
                    r_ = rl[:, nxt("rl", 2), :]
                    ACT(r_, bk, AF.Relu)
                    TT("gpsimd", hid[:, m, :], r_, r_, ALU.mult)
                for m in range(8):
                    w2 = W(i, l, 26 + m)
                    bk = gbank()
                    for kc in range(32):
                        MM(bk, w2[:, kc * 128:(kc + 1) * 128], hid[:, kc, :], kc == 0, kc == 31)
                    ACT(zf[:, m, :], bk, AF.Copy)
                    TT("gpsimd", sq[:, m, :], zf[:, m, :], zf[:, m, :], ALU.mult)
                post_norm_residual(l, 67, epsA)
            STORE("sync", outT.rearrange("(c p) t -> p c t", p=128)[:, :, blk], xres[:], "os")
        S.emit(final_wait_slots=["dbg0", "dbg1", "os"] + ["st%d" % k for k in range(NSTS)] + ["ws0", "ws1"])
    return nc


def _chunk_tile(Wm, cols):
    Kd = Wm.shape[0]
    kc = Kd // 128
    t = np.zeros((128, kc, 128), np.float32)
    sub = Wm[:, cols]
    t[:, :, :len(cols)] = sub.reshape(kc, 128, len(cols)).transpose(1, 0, 2)
    return t.reshape(128, kc * 128)


def _prep_weights(inp, L):
    w_in = inp["w_in"]
    wsl = np.zeros((L, NSLOT, 128, SLOTW), np.float32)
    misc = np.zeros((L, 128, NMISC), np.float32)
    par = np.zeros((L, 128, NPAR), np.float32)
    ar = np.arange
    for l in range(L):
        Wi = w_in[l]
        kr = list(range(384, 416)) + list(range(400, 416)) + list(range(384, 400))
        chunks = [ar(0, 128), ar(128, 256), ar(256, 384), np.array(kr),
                  ar(416, 544), ar(544, 672), ar(672, 800), ar(800, 928),
                  ar(1440, 1568), ar(1568, 1696),
                  ar(928, 1056), ar(1056, 1184),
                  ar(1184, 1312), ar(1312, 1440),
                  ar(1696, 1824), ar(1824, 1952)]
        for q, cols in enumerate(chunks):
            wsl[l, q // 4, :, (q % 4) * 1024:(q % 4 + 1) * 1024] = _chunk_tile(Wi, cols)
        brs = [inp["w_br_mla"][l], inp["w_br_sg"][l], inp["w_br_conv"][l], inp["w_br_pool"][l]]
        for bi in range(4):
            s = 4 + 3 * bi
            if bi == 0:
                wsl[l, s, 0:64, :] = brs[0].reshape(4, 64, 1024).transpose(1, 0, 2).reshape(64, 4096)
            else:
                wsl[l, s, :, 0:2048] = brs[bi].reshape(2, 128, 1024).transpose(1, 0, 2).reshape(128, 2048)
            for half in range(2):
                for m4 in range(4):
                    m = half * 4 + m4
                    c0 = 1952 + bi * 1024 + m * 128
                    wsl[l, s + 1 + half, :, m4 * 1024:(m4 + 1) * 1024] = _chunk_tile(Wi, ar(c0, c0 + 128))
        for m in range(8):
            wsl[l, 16 + m // 4, :, (m % 4) * 1024:(m % 4 + 1) * 1024] = _chunk_tile(inp["w_out"][l], ar(m * 128, (m + 1) * 128))
        for m in range(32):
            wsl[l, 18 + m // 4, :, (m % 4) * 1024:(m % 4 + 1) * 1024] = _chunk_tile(inp["w_ff1"][l], ar(m * 128, (m + 1) * 128))
        for m in range(8):
            wsl[l, 26 + m, :, :] = _chunk_tile(inp["w_ff2"][l], ar(m * 128, (m + 1) * 128))
        uq = inp["w_uq"][l]
        n1 = list(range(0, 64)) + list(range(96, 160))
        n2 = list(range(192, 256)) + list(range(288, 352))
        r1 = sum([list(range(h * 96 + 64, h * 96 + 96)) for h in range(4)], [])
        r2 = sum([list(range(h * 96 + 80, h * 96 + 96)) + list(range(h * 96 + 64, h * 96 + 80)) for h in range(4)], [])
        for j, cols in enumerate((n1, n2, r1, r2)):
            misc[l, :, j * 256:(j + 1) * 256] = _chunk_tile(uq, np.array(cols))
        ukv = inp["w_ukv"][l]
        for a in range(2):
            cols = list(range((2 * a) * 128, (2 * a) * 128 + 64)) + list(range((2 * a + 1) * 128, (2 * a + 1) * 128 + 64))
            misc[l, :, 1024 + a * 128:1024 + (a + 1) * 128] = ukv[:, cols]
        vcols = sum([list(range(h * 128 + 64, h * 128 + 128)) for h in range(4)], [])
        misc[l, :, 1280:1536] = ukv[:, vcols]
        for g in range(4):
            misc[l, :, 1536 + g * 128:1536 + (g + 1) * 128] = inp["sg_w"][l, g].T
        for a in range(2):
            blkd = np.zeros((128, 128), np.float32)
            blkd[0:64, 0:64] = inp["pool_w"][l, 2 * a]
            blkd[64:128, 64:128] = inp["pool_w"][l, 2 * a + 1]
            misc[l, :, 2048 + a * 128:2048 + (a + 1) * 128] = blkd
        col = lambda v: v.reshape(-1, 128).T
        par[l, :, 0:8] = col(inp["norm_mix_pre"][l])
        par[l, :, 8:40] = col(inp["gate_b"][l])
        par[l, :, 40:42] = col(inp["q_norm"][l])
        par[l, :, 42:43] = col(inp["kv_norm"][l])
        for k in range(3):
            par[l, :, 43 + 2 * k:45 + 2 * k] = col(inp["conv_w"][l, k])
        par[l, :, 49:51] = col(inp["pool_scale"][l])
        par[l, :, 51:59] = col(inp["norm_mix_post"][l])
        par[l, :, 59:67] = col(inp["norm_ffn_pre"][l])
        par[l, :, 67:75] = col(inp["norm_ffn_post"][l])
    lnp = np.zeros((L, 128, 768), np.float32)
    for l in range(L):
        lnp[l, :, 0:256] = inp["sg_ln_g"][l][None, :]
        lnp[l, :, 256:512] = inp["sg_ln_b"][l][None, :]
        for a in range(2):
            lnp[l, 0:64, 512 + a * 128:512 + (a + 1) * 128] = inp["sg_b"][l, 2 * a][None, :]
            lnp[l, 64:128, 512 + a * 128:512 + (a + 1) * 128] = inp["sg_b"][l, 2 * a + 1][None, :]
    cst = np.zeros((128, 8), np.float32)
    rows = np.arange(128)
    inv_freq = (10000.0 ** (-(np.arange(0, 32, 2, dtype=np.float32)) / 32.0)).astype(np.float32)
    cst[:, 0] = inv_freq[rows % 16]
    cst[:, 1] = np.where((rows % 32) < 16, -1.0, 1.0)
    cst[:, 2] = np.where(rows < 64, 2.0, 4.0)
    cst[:, 3] = np.where(rows < 64, 8.0, 16.0)
    return dict(
        wsl=wsl.reshape(L * NSLOT * 128, SLOTW), misc_in=misc.reshape(L * 128, NMISC),
        par_in=par.reshape(L * 128, NPAR),
        lnp_in=lnp.reshape(L * 128, 768), cst_in=cst)


_PROG_CACHE = {}


def run_module(inputs, T, L, ncores, ret_all=False):
    inp = {k: np.asarray(v) for k, v in inputs.items()}
    wts = _prep_weights(inp, L)
    key = (T, L)
    if key not in _PROG_CACHE:
        _PROG_CACHE[key] = build_program(T, L)
    nc = _PROG_CACHE[key]
    in_maps = []
    for b in range(ncores):
        m = dict(wts)
        m["xT"] = np.ascontiguousarray(inp["x"][b, :T].T.astype(np.float32))
        m["pos"] = np.ascontiguousarray(inp["positions"][b, :T].reshape(1, T).astype(np.int32))
        in_maps.append(m)
    res = run_bass_kernel_spmd(nc, in_maps, core_ids=list(range(ncores)))
    out = np.stack([np.ascontiguousarray(r["outT"].T) for r in res.results], 0)
    if ret_all:
        return out.astype(np.float32), res.results
    return out.astype(np.float32)


def kernel(**inputs):
    return run_module(inputs, SEQ, DEPTH, BATCH)
```

```python
import contextlib
import math
import numpy as np
import concourse.bass as bass
import concourse.mybir as mybir
from concourse.bass_utils import run_bass_kernel_spmd

F32 = mybir.dt.float32
BF16 = mybir.dt.bfloat16
I32 = mybir.dt.int32
AF = mybir.ActivationFunctionType
ALU = mybir.AluOpType
AX = mybir.AxisListType

D_MODEL = 1024
DEPTH = 4
SEQ = 8192
BATCH = 4
NB = 512
EPS = 1e-6
NSLOT = 34
SLOTW = 4096
NPAR = 80
NMISC = 2304
ATT_SCALE = 96.0 ** -0.5
GC1 = 0.7978845608028654
GC2 = 0.044715
TWO_PI = 2.0 * math.pi
CW1 = 6.28125
CW2 = TWO_PI - CW1
PI_SAFE = 3.1415925

ENGS = ["tensor", "vector", "scalar", "gpsimd", "sync"]
PAGE = 512


class _Op:
    __slots__ = ("eng", "fn", "deps", "is_dma", "slot", "seq", "sig", "users")

    def __init__(self, eng, fn, slot, seq):
        self.eng = eng
        self.fn = fn
        self.deps = []
        self.is_dma = slot is not None
        self.slot = slot
        self.seq = seq
        self.sig = None
        self.users = 0


def _keys_of(x):
    if not isinstance(x, bass.AP):
        return [x]
    ap = x.ap
    esz = 2 if x.dtype == BF16 else 4
    pstride = ap[0][0]
    off = x.offset
    fo = off % pstride if pstride > 0 else off
    ext = 0
    for st, cnt in ap[1:]:
        ext += abs(st) * (cnt - 1)
    b0 = fo * esz
    b1 = (fo + ext + 1) * esz - 1
    name = x.tensor.name
    return [(name, p) for p in range(b0 // PAGE, b1 // PAGE + 1)]


class Sched:
    def __init__(self, nc):
        self.nc = nc
        self.streams = {e: [] for e in ENGS}
        self.last_w = {}
        self.readers = {}
        self.last_dma = {}
        self.seq = 0

    def op(self, eng, fn, reads=(), writes=(), dma=None):
        self.seq += 1
        o = _Op(eng, fn, dma, self.seq)
        cand = {}

        def add(d):
            if d is None:
                return
            k = ("s", d.slot) if d.is_dma else ("e", d.eng)
            c = cand.get(k)
            if c is None or c.seq < d.seq:
                cand[k] = d

        rk = []
        for r in reads:
            rk.extend(_keys_of(r))
        wk = []
        for w in writes:
            wk.extend(_keys_of(w))
        for k in rk:
            add(self.last_w.get(k))
        for k in wk:
            add(self.last_w.get(k))
            rd = self.readers.get(k)
            if rd:
                for r in rd.values():
                    add(r)
        if dma is not None:
            add(self.last_dma.get(dma))
            self.last_dma[dma] = o
        for d in cand.values():
            if eng == "tensor" and (not d.is_dma) and d.eng == "tensor":
                continue
            o.deps.append(d)
            d.users += 1
        me = ("s", dma) if dma is not None else ("e", eng)
        for k in rk:
            rd = self.readers.get(k)
            if rd is None:
                rd = {}
                self.readers[k] = rd
            rd[me] = o
        for k in wk:
            self.last_w[k] = o
            self.readers[k] = {}
        self.streams[eng].append(o)
        return o

    def emit(self, final_wait_slots=()):
        nc = self.nc
        eng_cnt = {e: 0 for e in ENGS}
        slot_cnt = {}
        allops = []
        for e in ENGS:
            allops.extend(self.streams[e])
        allops.sort(key=lambda o: o.seq)
        for o in allops:
            if o.is_dma:
                slot_cnt[o.slot] = slot_cnt.get(o.slot, 0) + 16
                o.sig = slot_cnt[o.slot]
            elif o.users > 0:
                eng_cnt[o.eng] += 1
                o.sig = eng_cnt[o.eng]
        with contextlib.ExitStack() as st:
            esem = {e: st.enter_context(nc.semaphore("c_" + e)) for e in ENGS}
            ssem = {s: st.enter_context(nc.semaphore("d_" + str(s))) for s in slot_cnt}
            block = st.enter_context(nc.Block())

            def run_stream(e, engobj):
                known = {}
                for o in self.streams[e]:
                    need = {}
                    for d in o.deps:
                        key = ("s", d.slot) if d.is_dma else ("e", d.eng)
                        if need.get(key, 0) < d.sig:
                            need[key] = d.sig
                    for key, v in need.items():
                        if known.get(key, 0) >= v:
                            continue
                        known[key] = v
                        sem = ssem[key[1]] if key[0] == "s" else esem[key[1]]
                        engobj.wait_ge(sem, v)
                    ins = o.fn(engobj)
                    if o.is_dma:
                        ins.then_inc(ssem[o.slot], 16)
                    elif o.sig is not None:
                        ins.then_inc(esem[e], 1)
                if e == "sync":
                    for s in final_wait_slots:
                        if s in slot_cnt:
                            engobj.wait_ge(ssem[s], slot_cnt[s])

            @block.tensor
            def _(eng):
                run_stream("tensor", eng)

            @block.vector
            def _(eng):
                run_stream("vector", eng)

            @block.scalar
            def _(eng):
                run_stream("scalar", eng)

            @block.gpsimd
            def _(eng):
                run_stream("gpsimd", eng)

            @block.sync
            def _(eng):
                run_stream("sync", eng)


def build_program(T, L, dbg=None):
    NBLK = T // NB
    nc = bass.Bass("TRN2", target_bir_lowering=False)
    dt_in = lambda n, s, d: nc.dram_tensor(n, s, d, kind="ExternalInput").ap()
    xT = dt_in("xT", [D_MODEL, T], F32)
    pos = dt_in("pos", [1, T], I32)
    wsl = dt_in("wsl", [L * NSLOT * 128, SLOTW], F32)
    par_d = dt_in("par_in", [L * 128, NPAR], F32)
    misc_d = dt_in("misc_in", [L * 128, NMISC], F32)
    lnp_d = dt_in("lnp_in", [L * 128, 768], F32)
    cst_d = dt_in("cst_in", [128, 8], F32)
    outT = nc.dram_tensor("outT", [D_MODEL, T], F32, kind="ExternalOutput").ap()
    wscr = nc.dram_tensor("wscr", [L * NSLOT * 128, SLOTW], BF16, kind="Internal").ap()
    kscr = nc.dram_tensor("kscr", [L * 4 * 96, T], BF16, kind="Internal").ap()
    vscr = nc.dram_tensor("vscr", [L * NBLK * 128, 2048], BF16, kind="Internal").ap()
    csscr = nc.dram_tensor("csscr", [128, 2 * T], F32, kind="Internal").ap()
    mscr = nc.dram_tensor("mscr", [L * 128, NMISC], BF16, kind="Internal").ap()

    S = Sched(nc)
    st = contextlib.ExitStack()
    with st:
        def sb(name, shape, dt):
            return st.enter_context(nc.sbuf_tensor(name, shape, dt))

        ps = st.enter_context(nc.psum_tensor("ps", [128, 8, 512], F32))

        xres = sb("xres", [128, 8, 512], F32)
        xn = sb("xn", [128, 8, 512], BF16)
        sq = sb("sq", [128, 8, 512], BF16)
        NRING = 5
        wring = sb("wring", [128, NRING, SLOTW], BF16)
        cs = sb("cs", [128, 2, 512], F32)
        par = sb("par", [128, L, NPAR], F32)
        hgb = sb("hgb", [128, L, 32], F32)
        mwt = sb("mwt", [128, NMISC], BF16)
        lnp = sb("lnp", [128, 768], F32)
        cst = sb("cst", [128, 8], F32)
        ones = sb("ones", [128, 128], BF16)
        tri = sb("tri", [128, 128], BF16)
        epsA = sb("epsA", [128, 1], F32)
        eps4 = sb("eps4", [128, 1], F32)
        zh = sb("zh", [128, L, 2, 2], F32)
        ph = sb("ph", [128, L, 2, 15], F32)
        icn = sb("icn", [128, 2, 16], F32)
        Rt = sb("Rt", [128, 4, 512], F32)
        FA = sb("FA", [128, 16960], F32)
        BA = sb("BA", [128, 17 * 1024], BF16)

        def fa(off_k, shape):
            n = int(np.prod(shape))
            v = FA[:, off_k:off_k + n]
            if len(shape) == 1:
                return v
            if len(shape) == 2:
                return v.rearrange("p (a b) -> p a b", b=shape[1])
            return v.rearrange("p (a b c) -> p a b c", b=shape[1], c=shape[2])

        def ba(off_k, shape):
            n = int(np.prod(shape))
            v = BA[:, off_k:off_k + n]
            if len(shape) == 1:
                return v
            if len(shape) == 2:
                return v.rearrange("p (a b) -> p a b", b=shape[1])
            return v.rearrange("p (a b c) -> p a b c", b=shape[1], c=shape[2])

        K = 1024
        macc = fa(0, [8, 512])
        zf = macc
        tht = fa(4 * K, [2, 512])
        gtmp = fa(5 * K, [2, 512])
        rl = fa(6 * K, [2, 512])
        cqf = fa(0, [2, 512])
        ckvf = fa(1 * K, [512])
        krt1 = fa(1 * K + 512, [512])
        krA = fa(2 * K, [512])
        qr1 = fa(2 * K + 512, [512])
        qr2 = fa(3 * K, [512])
        qrb = fa(3 * K + 512, [512])
        xh = fa(4 * K, [2, 512])
        gu = fa(5 * K, [2, 512])
        gt = fa(6 * K, [2, 512])
        V1 = fa(7 * K, [4, 256])
        V2 = fa(8 * K, [4, 256])
        V3 = fa(9 * K, [4, 256])
        cvc = fa(10 * K, [2, 512])
        zt = fa(11 * K, [2, 514])
        p0 = fa(12 * K + 8, [2, 527])
        pA = fa(13 * K + 64, [2, 527])
        pB = fa(14 * K + 128, [2, 527])
        sgt = fa(15 * K + 256, [512])
        rden = fa(16 * K, [512])
        vst = fa(16 * K + 512, [4, 8])
        hid = ba(0, [32, 512])
        mbf = ba(11 * K, [8, 512])
        NKV = 3
        kbuf = ba(0, [NKV, 512])
        NPT = 4
        ptile = ba(4 * K, [NPT, 512])
        qT = ba(6 * K, [4, 512])
        qn = ba(10 * K, [2, 512])
        ckvn = ba(11 * K, [512])
        sqk = ba(11 * K + 512, [512])
        sq2 = ba(12 * K, [2, 512])
        krb = ba(13 * K, [512])
        knb = ba(14 * K, [2, 512])
        vbuf = ba(1 * K + 512, [NKV, 512]).rearrange("p s (c o) -> p s c o", o=128)
        oT = ba(15 * K, [4, 512])
        vsb = sb("vsb", [128, 4, 4, 128], BF16)
        vnA = sb("vnA", [128, 4, 4, 128], BF16)
        brin = sb("brin", [128, 2, 512], BF16)
        ysg = ba(3 * K, [2, 512])
        ycv = ba(8 * K, [2, 512])
        ypl = brin[:, 0:2, :]
        pooled = ba(9 * K, [2, 512])

        gp_banks = [0, 1, 2, 3, 4, 5]
        s_banks = [3, 4, 5]
        o_banks = [6, 7]
        rot = {"gp": 0, "s": 0, "pt": 0, "kv": 0, "th": 0, "gt": 0, "rl": 0, "st": 0}

        def nxt(name, n):
            v = rot[name]
            rot[name] = (v + 1) % n
            return v

        def gbank():
            return ps[:, gp_banks[nxt("gp", len(gp_banks))], :]

        def MM(out, lhsT, rhs, start, stop):
            rd = [lhsT, rhs] if start else [lhsT, rhs, out]
            S.op("tensor", lambda e: e.matmul(out, lhsT, rhs, start=start, stop=stop),
                 reads=rd, writes=[out])

        def ACT(out, in_, func, scale=1.0, bias=None, extra_reads=()):
            rd = [in_] + list(extra_reads)
            if isinstance(scale, bass.AP):
                rd.append(scale)
            if bias is None:
                S.op("scalar", lambda e: e.activation(out=out, in_=in_, func=func, scale=scale),
                     reads=rd, writes=[out])
            else:
                rd.append(bias)
                S.op("scalar", lambda e: e.activation(out=out, in_=in_, func=func, scale=scale, bias=bias),
                     reads=rd, writes=[out])

        def TT(eng, out, in0, in1, op):
            S.op(eng, lambda e: e.tensor_tensor(out=out, in0=in0, in1=in1, op=op),
                 reads=[in0, in1], writes=[out])

        def TS(eng, out, in0, s1, op0, s2=None, op1=None):
            rd = [in0]
            if isinstance(s1, bass.AP):
                rd.append(s1)
            if isinstance(s2, bass.AP):
                rd.append(s2)
            if op1 is None:
                S.op(eng, lambda e: e.tensor_scalar(out=out, in0=in0, scalar1=s1, scalar2=None, op0=op0),
                     reads=rd, writes=[out])
            else:
                S.op(eng, lambda e: e.tensor_scalar(out=out, in0=in0, scalar1=s1, scalar2=s2, op0=op0, op1=op1),
                     reads=rd, writes=[out])

        def STT(eng, out, in0, scalar, in1, op0, op1):
            eng = "vector"
            rd = [in0, in1]
            if isinstance(scalar, bass.AP):
                rd.append(scalar)
            S.op(eng, lambda e: e.scalar_tensor_tensor(out=out, in0=in0, scalar=scalar, in1=in1, op0=op0, op1=op1),
                 reads=rd, writes=[out])

        def CP(eng, out, in_):
            if eng == "scalar":
                ACT(out, in_, AF.Copy)
            else:
                S.op(eng, lambda e: e.tensor_copy(out=out, in_=in_), reads=[in_], writes=[out])

        def MEMSET(eng, out, val):
            S.op(eng, lambda e: e.memset(out, val), writes=[out])

        def RECIP(out, in_):
            S.op("vector", lambda e: e.reciprocal(out=out, in_=in_), reads=[in_], writes=[out])

        def DMA(eng, out, in_, slot, reads=(), writes=()):
            S.op(eng, lambda e: e.dma_start(out=out, in_=in_), reads=list(reads), writes=list(writes), dma=slot)

        def LOAD(eng, out, in_, slot, dkey=None):
            DMA(eng, out, in_, slot, reads=([dkey] if dkey is not None else []), writes=[out])

        def STORE(eng, out, in_, slot, dkey=None):
            DMA(eng, out, in_, slot, reads=[in_], writes=([dkey] if dkey is not None else []))

        dbg_n = [0]

        def DBG(name, ap, i, l):
            if dbg is None or (i, l) != dbg:
                return
            shp = list(ap.shape)
            d = nc.dram_tensor("dbg_" + name, shp, ap.dtype, kind="ExternalOutput").ap()
            dbg_n[0] += 1
            STORE("sync", d, ap, "dbg%d" % (dbg_n[0] % 2))

        NSTS = 4

        def st_slot():
            return "st%d" % nxt("st", NSTS)

        LOAD("sync", cst[:], cst_d, "su0")
        for l in range(L):
            LOAD("sync", par[:, l, :], par_d[l * 128:(l + 1) * 128, :], "su1")
        MEMSET("vector", ones[:], 1.0)
        MEMSET("vector", epsA[:], EPS)
        MEMSET("vector", eps4[:], 4.0 * EPS)
        MEMSET("vector", zh[:], 0.0)
        MEMSET("vector", ph[:], 0.0)
        MEMSET("gpsimd", BA[:], 0.0)
        MEMSET("gpsimd", FA[:], 0.0)
        trf = Rt[:, 0, 0:128]
        MEMSET("gpsimd", trf, 1.0)
        S.op("gpsimd", lambda e: e.affine_select(out=trf, in_=trf, pattern=[[1, 128]], compare_op=ALU.is_ge,
                                                 fill=0.0, base=0, channel_multiplier=-1),
             reads=[trf], writes=[trf])
        CP("vector", tri[:], trf)
        MEMSET("gpsimd", vnA[:], 0.0)
        MEMSET("gpsimd", vsb[:], 0.0)
        for s_ in range(NKV):
            pass
        MEMSET("gpsimd", vsb[:, :, :, 64:128], 1.0)
        for l in range(L):
            TS("vector", hgb[:, l, :], par[:, l, 8:40], 0.5, ALU.mult)
            LOAD("gpsimd", mwt[:], misc_d[l * 128:(l + 1) * 128, :], "su2")
            for g in range(4):
                w_ = mwt[:, 1536 + g * 128:1536 + (g + 1) * 128]
                TT("gpsimd", w_, w_, tri[:], ALU.mult)
            STORE("sync", mscr[l * 128:(l + 1) * 128, :], mwt[:], st_slot(), dkey=("m", l))
        ici = Rt[:, 1, 0:16].bitcast(I32)
        S.op("gpsimd", lambda e: e.iota(ici, pattern=[[1, 16]], base=1, channel_multiplier=0),
             writes=[ici])
        icf = Rt[:, 2, 0:16]
        CP("vector", icf, ici)
        for a in range(2):
            TS("vector", icn[:, a, :], icf, cst[:, 2 + a:3 + a], ALU.min)
            RECIP(icn[:, a, :], icn[:, a, :])
        posi = Rt[:, 0, :].bitcast(I32)
        ang = Rt[:, 1, :]
        kk = Rt[:, 2, :]
        mm_ = Rt[:, 3, :]
        cstile = cs
        for c in range(NBLK):
            sl = slice(c * NB, (c + 1) * NB)
            LOAD("sync", posi, pos[0:1, sl].broadcast_to([128, NB]), "su2")
            CP("vector", ang, posi)
            TS("vector", ang, ang, cst[:, 0:1], ALU.mult)
            TS("vector", kk, ang, 1.0 / TWO_PI, ALU.mult, 12582912.0, ALU.add)
            TS("vector", kk, kk, 12582912.0, ALU.subtract)
            STT("vector", ang, kk, -CW1, ang, ALU.mult, ALU.add)
            STT("vector", ang, kk, -CW2, ang, ALU.mult, ALU.add)
            for sgn in (1.0, -1.0):
                if sgn > 0:
                    TS("vector", mm_, ang, math.pi, ALU.is_gt, -TWO_PI, ALU.mult)
                else:
                    TS("vector", mm_, ang, -math.pi, ALU.is_lt, TWO_PI, ALU.mult)
                TT("vector", ang, ang, mm_, ALU.add)
            TS("vector", kk, ang, PI_SAFE, ALU.min, -PI_SAFE, ALU.max)
            ACT(cstile[:, 1, :], kk, AF.Sin)
            TS("vector", cstile[:, 1, :], cstile[:, 1, :], cst[:, 1:2], ALU.mult)
            TS("vector", kk, ang, math.pi / 2.0, ALU.add)
            TS("vector", mm_, kk, math.pi, ALU.is_gt, -TWO_PI, ALU.mult)
            TT("vector", kk, kk, mm_, ALU.add)
            TS("vector", kk, kk, PI_SAFE, ALU.min, -PI_SAFE, ALU.max)
            ACT(cstile[:, 0, :], kk, AF.Sin)
            STORE("sync", csscr[:, c * NB:(c + 1) * NB], cstile[:, 0, :], st_slot(), dkey=("cs", c))
            STORE("sync", csscr[:, T + c * NB:T + (c + 1) * NB], cstile[:, 1, :], st_slot(), dkey=("cs", c))

        wst = {"next": 0}

        def slot_region(s):
            if s == 4:
                return 64, SLOTW
            if s in (7, 10, 13):
                return 128, 2048
            return 128, SLOTW

        def issue_loads_upto(n):
            while wst["next"] <= n:
                m = wst["next"]
                bi_, rem = divmod(m, L * NSLOT)
                l_, s_ = divmod(rem, NSLOT)
                buf = wring[:, m % NRING, :]
                row0 = (l_ * NSLOT + s_) * 128
                if bi_ == 0:
                    LOAD("gpsimd", buf, wsl[row0:row0 + 128, :], "wl%d" % (m % NRING))
                    STORE("sync", wscr[row0:row0 + 128, :], buf, "ws%d" % (m % 2), dkey=("w", l_, s_))
                else:
                    pp, cc = slot_region(s_)
                    LOAD("sync", wring[0:pp, m % NRING, 0:cc], wscr[row0:row0 + pp, 0:cc],
                         "wl%d" % (m % NRING), dkey=("w", l_, s_))
                wst["next"] += 1

        def W(i, l, s, keep=0):
            n = (i * L + l) * NSLOT + s
            issue_loads_upto(min(n + NRING - 1 - keep, NBLK * L * NSLOT - 1))
            return wring[:, n % NRING, :]

        def rms_to_bf16(src, gcol0, l, dst, eps_t):
            for c in range(8):
                if c % 2 == 0:
                    ACT(sq[:, c, :], src[:, c, :], AF.Square)
                else:
                    TT("gpsimd", sq[:, c, :], src[:, c, :], src[:, c, :], ALU.mult)
            bk = gbank()
            for c in range(8):
                MM(bk, ones[:], sq[:, c, :], c == 0, c == 7)
            ACT(Rt[:, 0, :], bk, AF.Ln, scale=1.0 / D_MODEL, bias=eps_t[:])
            ACT(Rt[:, 1, :], Rt[:, 0, :], AF.Exp, scale=-0.5)
            for c in range(8):
                STT("vector" if c % 2 == 0 else "gpsimd", dst[:, c, :], src[:, c, :],
                    par[:, l, gcol0 + c:gcol0 + c + 1], Rt[:, 1, :], ALU.mult, ALU.mult)

        def post_norm_residual(l, gcol0, eps_t, to_zf=False):
            bk = gbank()
            for c in range(8):
                MM(bk, ones[:], sq[:, c, :], c == 0, c == 7)
            ACT(Rt[:, 0, :], bk, AF.Ln, scale=1.0 / D_MODEL, bias=eps_t[:])
            ACT(Rt[:, 1, :], Rt[:, 0, :], AF.Exp, scale=-0.5)
            for c in range(8):
                tmp = gtmp[:, c % 2, :]
                STT("vector", tmp, zf[:, c, :], par[:, l, gcol0 + c:gcol0 + c + 1], Rt[:, 1, :],
                    ALU.mult, ALU.mult)
                TT("gpsimd", (zf if to_zf else xres)[:, c, :], xres[:, c, :], tmp, ALU.add)

        def gelu_half(xh_ap, tmp_ap, out_ap):
            ACT(tmp_ap, xh_ap, AF.Square)
            TS("vector", tmp_ap, tmp_ap, 4.0 * GC2, ALU.mult, 1.0, ALU.add)
            TT("gpsimd", tmp_ap, tmp_ap, xh_ap, ALU.mult)
            ACT(tmp_ap, tmp_ap, AF.Tanh, scale=2.0 * GC1)
            STT("vector", out_ap, tmp_ap, 1.0, xh_ap, ALU.add, ALU.mult)

        for i in range(NBLK):
            blk = slice(i * NB, (i + 1) * NB)
            LOAD("sync", xres[:], xT.rearrange("(c p) t -> p c t", p=128)[:, :, blk], "xl")
            LOAD("sync", cs[:, 0, :], csscr[:, i * NB:(i + 1) * NB], "cl0", dkey=("cs", i))
            LOAD("sync", cs[:, 1, :], csscr[:, T + i * NB:T + (i + 1) * NB], "cl1", dkey=("cs", i))
            for l in range(L):
                P = lambda c0: par[:, l, c0:c0 + 1]
                LOAD("sync", mwt[:], mscr[l * 128:(l + 1) * 128, :], "ml", dkey=("m", l))
                LOAD("sync", lnp[:], lnp_d[l * 128:(l + 1) * 128, :], "pl")
                rms_to_bf16(xres, 0, l, xn, epsA)
                CP("gpsimd", zt[:, :, 0:2], zh[:, l, :, :])
                CP("gpsimd", p0[:, :, 0:15], ph[:, l, :, :])
                def wchunk(q):
                    wt = W(i, l, q // 4)
                    return wt[:, (q % 4) * 1024:(q % 4 + 1) * 1024].rearrange("p (k o) -> p k o", o=128)

                def proj_fm(q, M=128):
                    wt = wchunk(q)
                    bk = gbank()
                    for kc in range(8):
                        MM(bk[0:M, :], wt[:, kc, 0:M], xn[:, kc, :], kc == 0, kc == 7)
                    return bk

                for a in range(2):
                    bk = proj_fm(a)
                    ACT(cqf[:, a, :], bk, AF.Copy)
                    TT("gpsimd", sq2[:, a, :], cqf[:, a, :], cqf[:, a, :], ALU.mult)
                bk = proj_fm(2)
                ACT(ckvf, bk, AF.Copy)
                TT("gpsimd", sqk, ckvf, ckvf, ALU.mult)
                bk = proj_fm(3, M=64)
                TT("vector", krt1[0:32, :], bk[0:32, :], cs[0:32, 0, :], ALU.mult)
                ACT(krA[0:32, :], bk[32:64, :], AF.Copy)
                TT("gpsimd", krA[0:32, :], krA[0:32, :], cs[0:32, 1, :], ALU.mult)
                TT("gpsimd", krb[0:32, :], krt1[0:32, :], krA[0:32, :], ALU.add)
                bq = gbank()
                MM(bq, ones[:], sq2[:, 0, :], True, False)
                MM(bq, ones[:], sq2[:, 1, :], False, True)
                bkv = gbank()
                MM(bkv, ones[:], sqk, True, True)
                ACT(Rt[:, 0, :], bq, AF.Ln, scale=1.0 / 256.0, bias=epsA[:])
                ACT(Rt[:, 2, :], bkv, AF.Ln, scale=1.0 / 128.0, bias=epsA[:])
                ACT(Rt[:, 1, :], Rt[:, 0, :], AF.Exp, scale=-0.5)
                ACT(Rt[:, 3, :], Rt[:, 2, :], AF.Exp, scale=-0.5)
                for a in range(2):
                    STT("vector", qn[:, a, :], cqf[:, a, :], P(40 + a), Rt[:, 1, :], ALU.mult, ALU.mult)
                STT("vector", ckvn, ckvf, P(42), Rt[:, 3, :], ALU.mult, ALU.mult)
                for a in range(2):
                    bk = proj_fm(4 + a)
                    ACT(xh[:, a, :], bk, AF.Identity, scale=0.5)
                    gelu_half(xh[:, a, :], gt[:, a, :], gu[:, a, :])
                wv0 = wchunk(6)
                wv1 = wchunk(7)
                for tcp in range(2):
                    bk = gbank().rearrange("p (c f) -> p c f", f=256)
                    for tcl in range(2):
                        tc = 2 * tcp + tcl
                        for a, wv in ((0, wv0), (1, wv1)):
                            for kc in range(8):
                                MM(bk[:, tcl, a * 128:(a + 1) * 128], xn[:, kc, tc * 128:(tc + 1) * 128],
                                   wv[:, kc, :], kc == 0, kc == 7)
                    ACT(V1[:, 2 * tcp:2 * tcp + 2, :], bk, AF.Identity, scale=0.5)
                    gelu_half(V1[:, 2 * tcp:2 * tcp + 2, :], V2[:, 2 * tcp:2 * tcp + 2, :],
                              V3[:, 2 * tcp:2 * tcp + 2, :])
                for a in range(2):
                    bk = proj_fm(8 + a)
                    ACT(cvc[:, a, :], bk, AF.Copy)
                for a in range(2):
                    bk = proj_fm(10 + a)
                    TT("vector", zt[:, a, 2:514], bk, cvc[:, a, :], ALU.mult)
                for a in range(2):
                    TS("vector", cvc[:, a, :], zt[:, a, 2:514], P(43 + 4 + a), ALU.mult)
                    STT("vector", cvc[:, a, :], zt[:, a, 1:513], P(43 + 2 + a), cvc[:, a, :], ALU.mult, ALU.add)
                    STT("gpsimd", cvc[:, a, :], zt[:, a, 0:512], P(43 + a), cvc[:, a, :], ALU.mult, ALU.add)
                    bk = proj_fm(12 + a)
                    TT("vector", ycv[:, a, :], bk, cvc[:, a, :], ALU.mult)
                CP("gpsimd", zh[:, l, :, :], zt[:, :, 512:514])
                for a in range(2):
                    bk = proj_fm(14 + a)
                    ACT(p0[:, a, 15:527], bk, AF.Copy)
                TT("gpsimd", pA[:, :, 1:527], p0[:, :, 1:527], p0[:, :, 0:526], ALU.add)
                TT("gpsimd", pB[:, :, 3:527], pA[:, :, 3:527], pA[:, :, 1:525], ALU.add)
                TT("gpsimd", pA[:, 1, 7:527], pB[:, 1, 7:527], pB[:, 1, 3:523], ALU.add)
                TT("gpsimd", pB[:, 1, 15:527], pA[:, 1, 15:527], pA[:, 1, 7:519], ALU.add)
                for a in range(2):
                    for e2 in range(2):
                        rows = slice(e2 * 64, (e2 + 1) * 64)
                        src = (pA if e2 == 0 else pB)
                        win = (2, 4, 8, 16)[2 * a + e2]
                        STT("vector", pooled[rows, a, :], src[rows, a, 15:527], 1.0 / win,
                            p0[rows, a, 15:527], ALU.mult, ALU.subtract)
                        if i == 0:
                            t16 = vst[rows, 0:2, :].rearrange("p a b -> p (a b)")
                            TT("vector", t16, src[rows, a, 15:31], icn[rows, a, :], ALU.mult)
                            TT("vector", pooled[rows, a, 0:16], t16, p0[rows, a, 15:31], ALU.subtract)
                CP("gpsimd", ph[:, l, :, :], p0[:, :, 512:527])
                def late_pool_mix():
                    for a in range(2):
                        bk = gbank()
                        MM(bk, mwt[:, 2048 + a * 128:2048 + (a + 1) * 128], pooled[:, a, :], True, True)
                        TS("vector", ypl[:, a, :], bk, P(49 + a), ALU.mult)
                def late_sg():
                    for a in range(2):
                        bk = gbank()
                        for tc in range(4):
                            for e2 in range(2):
                                g = 2 * a + e2
                                MM(bk[:, tc * 128:(tc + 1) * 128], vnA[:, tc, g, :],
                                   mwt[:, 1536 + g * 128:1536 + (g + 1) * 128], e2 == 0, e2 == 1)
                        TT("vector", sgt.rearrange("p (c t) -> p c t", t=128), bk.rearrange("p (c t) -> p c t", t=128),
                           lnp[:, 512 + a * 128:512 + (a + 1) * 128].unsqueeze(1).broadcast_to([128, 4, 128]), ALU.add)
                        TT("gpsimd", ysg[:, a, :], sgt, gu[:, a, :], ALU.mult)
                DBG("xn", xn[:], i, l)
                DBG("gu", gu, i, l)
                DBG("vnA", vnA[:], i, l)
                DBG("qn", qn, i, l)
                DBG("ckvn", ckvn, i, l)
                qb = []
                for j in range(4):
                    bk = ps[:, [0, 1, 2, 6][j], :]
                    for kc in range(2):
                        MM(bk, mwt[:, j * 256 + kc * 128:j * 256 + (kc + 1) * 128], qn[:, kc, :], kc == 0, kc == 1)
                    qb.append(bk)
                for jn in range(2):
                    for e2 in range(2):
                        CP("scalar" if e2 == 0 else "vector", qT[0:64, 2 * jn + e2, :], qb[jn][e2 * 64:(e2 + 1) * 64, :])
                TT("vector", qr1, qb[2], cs[:, 0, :], ALU.mult)
                TT("vector", qr2, qb[3], cs[:, 1, :], ALU.mult)
                TT("gpsimd", qrb, qr1, qr2, ALU.add)
                for h in range(4):
                    CP("scalar" if h % 2 == 0 else "vector", qT[64:96, h, :], qrb[h * 32:(h + 1) * 32, :])
                for a in range(2):
                    bk = gbank()
                    MM(bk, mwt[:, 1024 + a * 128:1024 + (a + 1) * 128], ckvn, True, True)
                    ACT(knb[:, a, :], bk, AF.Copy)
                for tcp in range(2):
                    bk = gbank().rearrange("p (c f) -> p c f", f=256)
                    for tcl in range(2):
                        tc = 2 * tcp + tcl
                        MM(bk[:, tcl, :], ckvn[:, tc * 128:(tc + 1) * 128], mwt[:, 1280:1536], True, True)
                    for tcl in range(2):
                        tc = 2 * tcp + tcl
                        CP("vector", vsb[:, :, tc, 0:64], bk[:, tcl, :].rearrange("p (h d) -> p h d", d=64))
                groups = [(h, j) for h in range(4) for j in range(i + 1)]
                gslot = {}
                gnext = [0]

                def issue_group_loads(upto):
                    while gnext[0] <= upto and gnext[0] < len(groups):
                        h_, j_ = groups[gnext[0]]
                        sl2 = nxt("kv", NKV)
                        gslot[(h_, j_)] = sl2
                        r0_ = (l * 4 + h_) * 96
                        LOAD("sync", kbuf[0:96, sl2, :], kscr[r0_:r0_ + 96, j_ * NB:(j_ + 1) * NB], "kl%d" % sl2,
                             dkey=("k", l, h_, j_))
                        vr = (l * NBLK + j_) * 128
                        LOAD("sync", vbuf[:, sl2, :, :].rearrange("p c o -> p (c o)"),
                             vscr[vr:vr + 128, h_ * 512:(h_ + 1) * 512],
                             "vl%d" % sl2, dkey=("v", l, j_))
                        gnext[0] += 1

                npre = 0
                while npre < NKV - 1 and npre < len(groups) and groups[npre][1] < i:
                    npre += 1
                if npre > 0:
                    issue_group_loads(npre - 1)
                for h in range(4):
                    r0 = (l * 4 + h) * 96
                    STORE("sync", kscr[r0:r0 + 64, blk], knb[(h % 2) * 64:(h % 2 + 1) * 64, h // 2, :], st_slot(),
                          dkey=("k", l, h, i))
                    STORE("sync", kscr[r0 + 64:r0 + 96, blk], krb[0:32, :], st_slot(), dkey=("k", l, h, i))
                vr0 = (l * NBLK + i) * 128
                STORE("sync", vscr[vr0:vr0 + 128, :], vsb[:].rearrange("p h c d -> p (h c d)"), st_slot(),
                      dkey=("v", l, i))
                DBG("qT", qT[0:96], i, l)
                DBG("knb", knb, i, l)
                DBG("krb", krb[0:32], i, l)
                mu = vst[:, 0, 0:4]
                var = vst[:, 1, 0:4]
                rsd = vst[:, 2, 0:4]
                S.op("vector", lambda e: e.tensor_reduce(out=mu, in_=V3[:, :, :], axis=AX.X, op=ALU.add),
                     reads=[V3[:, :, :]], writes=[mu])
                TS("vector", mu, mu, 1.0 / 256.0, ALU.mult)
                for tc in range(4):
                    TS("vector", V1[:, tc, :], V3[:, tc, :], mu[:, tc:tc + 1], ALU.subtract)
                TT("gpsimd", V2[:, :, :], V1[:, :, :], V1[:, :, :], ALU.mult)
                S.op("vector", lambda e: e.tensor_reduce(out=var, in_=V2[:, :, :], axis=AX.X, op=ALU.add),
                     reads=[V2[:, :, :]], writes=[var])
                ACT(rsd, var, AF.Ln, scale=1.0 / 256.0, bias=epsA[:])
                ACT(rsd, rsd, AF.Exp, scale=-0.5)
                for tc in range(4):
                    STT("vector", V2[:, tc, :], V1[:, tc, :], rsd[:, tc:tc + 1], lnp[:, 0:256], ALU.mult, ALU.mult)
                    for e2 in range(2):
                        o_ = vnA[:, tc, :, :].rearrange("p (a e) o -> p a e o", e=2)[:, :, e2, e2 * 64:(e2 + 1) * 64]
                        i0 = V2[:, tc, :].rearrange("p (a e c) -> p a e c", e=2, c=64)[:, :, e2, :]
                        i1 = lnp[:, 256:512].rearrange("p (a e c) -> p a e c", e=2, c=64)[:, :, e2, :]
                        TT("gpsimd", o_, i0, i1, ALU.add)
                steps = [(h, j, c) for h in range(4) for j in range(i + 1) for c in range(4)]

                LA = 2
                pend = {}
                for idx in range(len(steps) + LA):
                    if idx < len(steps):
                        h, j, c = steps[idx]
                        if c == 0:
                            issue_group_loads(groups.index((h, j)) + 1)
                        sl_ = gslot[(h, j)]
                        q0 = c * 128 if j == i else 0
                        sbk = ps[:, s_banks[nxt("s", len(s_banks))], :]
                        MM(sbk[:, q0:512], kbuf[0:96, sl_, c * 128:(c + 1) * 128], qT[0:96, h, q0:512], True, True)
                        pt = ptile[:, nxt("pt", NPT), :]
                        ACT(pt[:, q0:512], sbk[:, q0:512], AF.Exp, scale=ATT_SCALE)
                        if j == i:
                            TT("gpsimd", pt[:, q0:q0 + 128], pt[:, q0:q0 + 128], tri[:], ALU.mult)
                        pend[idx] = (sl_, q0, pt)
                    if idx >= LA:
                        h, j, c = steps[idx - LA]
                        sl_, q0, pt = pend.pop(idx - LA)
                        ob = ps[:, o_banks[h % 2], :]
                        MM(ob[:, q0:512], vbuf[:, sl_, c, :], pt[:, q0:512], (j == 0 and c == 0),
                           (j == i and c == 3))
                        if j == i and c == 3:
                            RECIP(rden[0:64, :], ob[64:128, :])
                            TT("vector", oT[0:64, h, :], ob[0:64, :], rden[0:64, :], ALU.mult)
                late_pool_mix()
                late_sg()
                DBG("oT", oT[0:64], i, l)
                DBG("ycv", ycv, i, l)
                DBG("ypl", ypl, i, l)
                DBG("ysg", ysg, i, l)
                for bi in range(4):
                    brw = W(i, l, 4 + 3 * bi)
                    for half in range(2):
                        gw = W(i, l, 4 + 3 * bi + 1 + half, keep=1 + half)
                        for m4 in range(4):
                            m = half * 4 + m4
                            gbk = gbank()
                            for kc in range(8):
                                MM(gbk, gw[:, m4 * 1024 + kc * 128:m4 * 1024 + (kc + 1) * 128], xn[:, kc, :],
                                   kc == 0, kc == 7)
                            ybk = gbank()
                            if bi == 0:
                                for h in range(4):
                                    MM(ybk, brw[0:64, h * 1024 + m * 128:h * 1024 + (m + 1) * 128], oT[0:64, h, :],
                                       h == 0, h == 3)
                            else:
                                bin_ = (None, ysg, ycv, ypl)[bi]
                                for kc in range(2):
                                    MM(ybk, brw[:, kc * 1024 + m * 128:kc * 1024 + (m + 1) * 128], bin_[:, kc, :],
                                       kc == 0, kc == 1)
                            th = tht[:, nxt("th", 2), :]
                            ACT(th, gbk, AF.Tanh, scale=0.5, bias=hgb[:, l, bi * 8 + m:bi * 8 + m + 1])
                            if bi == 0:
                                STT("vector", macc[:, m, :], th, 1.0, ybk, ALU.add, ALU.mult)
                            else:
                                tmp = gtmp[:, nxt("gt", 2), :]
                                STT("vector", tmp, th, 1.0, ybk, ALU.add, ALU.mult)
                                if bi < 3:
                                    TT("gpsimd", macc[:, m, :], macc[:, m, :], tmp, ALU.add)
                                else:
                                    TT("gpsimd", mbf[:, m, :], macc[:, m, :], tmp, ALU.add)
                DBG("mbf", mbf, i, l)
                for m in range(8):
                    wo = W(i, l, 16 + m // 4)
                    bk = gbank()
                    for kc in range(8):
                        MM(bk, wo[:, (m % 4) * 1024 + kc * 128:(m % 4) * 1024 + (kc + 1) * 128], mbf[:, kc, :],
                           kc == 0, kc == 7)
                    ACT(zf[:, m, :], bk, AF.Copy)
                    TT("gpsimd", sq[:, m, :], zf[:, m, :], zf[:, m, :], ALU.mult)
                post_norm_residual(l, 51, eps4)
                DBG("xmid", xres[:], i, l)
                rms_to_bf16(xres, 59, l, xn, epsA)
                for m in range(32):
                    w1 = W(i, l, 18 + m // 4)
                    bk = gbank()
                    for kc in range(8):
                        MM(bk, w1[:, (m % 4) * 1024 + kc * 128:(m % 4) * 1024 + (kc + 1) * 128], xn[:, kc, :],
                           kc == 0, kc == 7)
                    r_ = rl[:, nxt("rl", 2), :]
                    ACT(r_, bk, AF.Relu)
                    TT("gpsimd", hid[:, m, :], r_, r_, ALU.mult)
                for m in range(8):
                    w2 = W(i, l, 26 + m)
                    bk = gbank()
                    for kc in range(32):
                        MM(bk, w2[:, kc * 128:(kc + 1) * 128], hid[:, kc, :], kc == 0, kc == 31)
                    ACT(zf[:, m, :], bk, AF.Copy)
                    TT("gpsimd", sq[:, m, :], zf[:, m, :], zf[:, m, :], ALU.mult)
                post_norm_residual(l, 67, epsA, to_zf=(l == L - 1))
            STORE("sync", outT.rearrange("(c p) t -> p c t", p=128)[:, :, blk], zf, "os")
        S.emit(final_wait_slots=["dbg0", "dbg1", "os"] + ["st%d" % k for k in range(NSTS)] + ["ws0", "ws1"])
    return nc


def _chunk_tile(Wm, cols):
    Kd = Wm.shape[0]
    kc = Kd // 128
    t = np.zeros((128, kc, 128), np.float32)
    sub = Wm[:, cols]
    t[:, :, :len(cols)] = sub.reshape(kc, 128, len(cols)).transpose(1, 0, 2)
    return t.reshape(128, kc * 128)


def _prep_weights(inp, L):
    w_in = inp["w_in"]
    wsl = np.zeros((L, NSLOT, 128, SLOTW), np.float32)
    misc = np.zeros((L, 128, NMISC), np.float32)
    par = np.zeros((L, 128, NPAR), np.float32)
    ar = np.arange
    for l in range(L):
        Wi = w_in[l]
        kr = list(range(384, 416)) + list(range(400, 416)) + list(range(384, 400))
        chunks = [ar(0, 128), ar(128, 256), ar(256, 384), np.array(kr),
                  ar(416, 544), ar(544, 672), ar(672, 800), ar(800, 928),
                  ar(1440, 1568), ar(1568, 1696),
                  ar(928, 1056), ar(1056, 1184),
                  ar(1184, 1312), ar(1312, 1440),
                  ar(1696, 1824), ar(1824, 1952)]
        for q, cols in enumerate(chunks):
            wsl[l, q // 4, :, (q % 4) * 1024:(q % 4 + 1) * 1024] = _chunk_tile(Wi, cols)
        brs = [inp["w_br_mla"][l], inp["w_br_sg"][l], inp["w_br_conv"][l], inp["w_br_pool"][l]]
        for bi in range(4):
            s = 4 + 3 * bi
            if bi == 0:
                wsl[l, s, 0:64, :] = brs[0].reshape(4, 64, 1024).transpose(1, 0, 2).reshape(64, 4096)
            else:
                wsl[l, s, :, 0:2048] = brs[bi].reshape(2, 128, 1024).transpose(1, 0, 2).reshape(128, 2048)
            for half in range(2):
                for m4 in range(4):
                    m = half * 4 + m4
                    c0 = 1952 + bi * 1024 + m * 128
                    wsl[l, s + 1 + half, :, m4 * 1024:(m4 + 1) * 1024] = _chunk_tile(Wi, ar(c0, c0 + 128))
        for m in range(8):
            wsl[l, 16 + m // 4, :, (m % 4) * 1024:(m % 4 + 1) * 1024] = _chunk_tile(inp["w_out"][l], ar(m * 128, (m + 1) * 128))
        for m in range(32):
            wsl[l, 18 + m // 4, :, (m % 4) * 1024:(m % 4 + 1) * 1024] = _chunk_tile(inp["w_ff1"][l], ar(m * 128, (m + 1) * 128))
        for m in range(8):
            wsl[l, 26 + m, :, :] = _chunk_tile(inp["w_ff2"][l], ar(m * 128, (m + 1) * 128))
        uq = inp["w_uq"][l]
        n1 = list(range(0, 64)) + list(range(96, 160))
        n2 = list(range(192, 256)) + list(range(288, 352))
        r1 = sum([list(range(h * 96 + 64, h * 96 + 96)) for h in range(4)], [])
        r2 = sum([list(range(h * 96 + 80, h * 96 + 96)) + list(range(h * 96 + 64, h * 96 + 80)) for h in range(4)], [])
        for j, cols in enumerate((n1, n2, r1, r2)):
            misc[l, :, j * 256:(j + 1) * 256] = _chunk_tile(uq, np.array(cols))
        ukv = inp["w_ukv"][l]
        for a in range(2):
            cols = list(range((2 * a) * 128, (2 * a) * 128 + 64)) + list(range((2 * a + 1) * 128, (2 * a + 1) * 128 + 64))
            misc[l, :, 1024 + a * 128:1024 + (a + 1) * 128] = ukv[:, cols]
        vcols = sum([list(range(h * 128 + 64, h * 128 + 128)) for h in range(4)], [])
        misc[l, :, 1280:1536] = ukv[:, vcols]
        for g in range(4):
            misc[l, :, 1536 + g * 128:1536 + (g + 1) * 128] = inp["sg_w"][l, g].T
        for a in range(2):
            blkd = np.zeros((128, 128), np.float32)
            blkd[0:64, 0:64] = inp["pool_w"][l, 2 * a]
            blkd[64:128, 64:128] = inp["pool_w"][l, 2 * a + 1]
            misc[l, :, 2048 + a * 128:2048 + (a + 1) * 128] = blkd
        col = lambda v: v.reshape(-1, 128).T
        par[l, :, 0:8] = col(inp["norm_mix_pre"][l])
        par[l, :, 8:40] = col(inp["gate_b"][l])
        par[l, :, 40:42] = col(inp["q_norm"][l])
        par[l, :, 42:43] = col(inp["kv_norm"][l])
        for k in range(3):
            par[l, :, 43 + 2 * k:45 + 2 * k] = col(inp["conv_w"][l, k])
        par[l, :, 49:51] = col(inp["pool_scale"][l])
        par[l, :, 51:59] = col(inp["norm_mix_post"][l])
        par[l, :, 59:67] = col(inp["norm_ffn_pre"][l])
        par[l, :, 67:75] = col(inp["norm_ffn_post"][l])
    lnp = np.zeros((L, 128, 768), np.float32)
    for l in range(L):
        lnp[l, :, 0:256] = inp["sg_ln_g"][l][None, :]
        lnp[l, :, 256:512] = inp["sg_ln_b"][l][None, :]
        for a in range(2):
            lnp[l, 0:64, 512 + a * 128:512 + (a + 1) * 128] = inp["sg_b"][l, 2 * a][None, :]
            lnp[l, 64:128, 512 + a * 128:512 + (a + 1) * 128] = inp["sg_b"][l, 2 * a + 1][None, :]
    cst = np.zeros((128, 8), np.float32)
    rows = np.arange(128)
    inv_freq = (10000.0 ** (-(np.arange(0, 32, 2, dtype=np.float32)) / 32.0)).astype(np.float32)
    cst[:, 0] = inv_freq[rows % 16]
    cst[:, 1] = np.where((rows % 32) < 16, -1.0, 1.0)
    cst[:, 2] = np.where(rows < 64, 2.0, 4.0)
    cst[:, 3] = np.where(rows < 64, 8.0, 16.0)
    return dict(
        wsl=wsl.reshape(L * NSLOT * 128, SLOTW), misc_in=misc.reshape(L * 128, NMISC),
        par_in=par.reshape(L * 128, NPAR),
        lnp_in=lnp.reshape(L * 128, 768), cst_in=cst)


_PROG_CACHE = {}


def run_module(inputs, T, L, ncores, ret_all=False):
    inp = {k: np.asarray(v) for k, v in inputs.items()}
    wts = _prep_weights(inp, L)
    key = (T, L)
    if key not in _PROG_CACHE:
        _PROG_CACHE[key] = build_program(T, L)
    nc = _PROG_CACHE[key]
    in_maps = []
    for b in range(ncores):
        m = dict(wts)
        m["xT"] = np.ascontiguousarray(inp["x"][b, :T].T.astype(np.float32))
        m["pos"] = np.ascontiguousarray(inp["positions"][b, :T].reshape(1, T).astype(np.int32))
        in_maps.append(m)
    res = run_bass_kernel_spmd(nc, in_maps, core_ids=list(range(ncores)))
    out = np.stack([np.ascontiguousarray(r["outT"].T) for r in res.results], 0)
    if ret_all:
        return out.astype(np.float32), res.results
    return out.astype(np.float32)


def kernel(**inputs):
    return run_module(inputs, SEQ, DEPTH, BATCH)
```
